# Optimizing a Trainium2 kernel written in Bass

```python
import jax, jax.numpy as jnp
from jax import lax
import numpy as np

D_MODEL = 1024
BATCH = 8
SEQ = 4096
DEPTH = 1

HEAD_DIM = 64
DIL_PAIRS = ((128, 1), (512, 4), (2048, 16))
N_DIL_GROUPS = len(DIL_PAIRS)
HEADS_PER_DIL_GROUP = 4
N_DIL_HEADS = N_DIL_GROUPS * HEADS_PER_DIL_GROUP
DIL_WIDTH = N_DIL_HEADS * HEAD_DIM
DIL_OUT_WIDTH = HEADS_PER_DIL_GROUP * HEAD_DIM
N_FOX_HEADS = 8
FOX_WIDTH = N_FOX_HEADS * HEAD_DIM
FOX_BLOCK = 128
N_BRANCHES = 2
D_FF = -(-8 * D_MODEL // (3 * 256)) * 256
RMS_EPS = 1e-6
NEG_INF = -1e30
ATTN_SCALE = HEAD_DIM ** -0.5
IN_COLS = 3 * DIL_WIDTH + 3 * FOX_WIDTH + N_FOX_HEADS + N_BRANCHES * D_MODEL

kernel_name = 'hybrid_dilated_fox_gated_block'


def rms_norm(x, g):
    xf = x.astype(jnp.float32)
    y = xf * lax.rsqrt(jnp.mean(xf * xf, axis=-1, keepdims=True) + RMS_EPS)
    return (y * g.astype(jnp.float32)).astype(x.dtype)


def alibi_slopes(n):
    return 2.0 ** (-8.0 * jnp.arange(1, n + 1, dtype=jnp.float32) / n)


def dilated_window_attention(q, k, v, window, dilation, slopes):
    b, s, h, dh = q.shape
    w = window // dilation
    span = w * dilation
    s_pad = -(-s // span) * span
    nb = s_pad // span
    pad = ((0, 0), (0, s_pad - s), (0, 0), (0, 0))

    def to_blocks(t):
        return jnp.pad(t, pad).reshape(b, nb, w, dilation, h, dh)

    def with_prev(t):
        prev = jnp.pad(t, ((0, 0), (1, 0), (0, 0), (0, 0), (0, 0), (0, 0)))[:, :-1]
        return jnp.concatenate([prev, t], axis=2)

    qb = to_blocks(q)
    kw = with_prev(to_blocks(k))
    vw = with_prev(to_blocks(v))
    scores = jnp.einsum('bnqrhd,bnkrhd->bnrhqk', qb, kw,
                        preferred_element_type=jnp.float32) * ATTN_SCALE
    qi = jnp.arange(w)[:, None]
    ki = jnp.arange(2 * w)[None, :] - w
    rel = qi - ki
    glob = jnp.arange(nb)[:, None] * w + ki
    valid = ((rel >= 0) & (rel <= w))[None] & (glob >= 0)[:, None, :]
    bias = -slopes[:, None, None] * (rel * dilation).astype(jnp.float32)[None]
    logits = jnp.where(valid[None, :, None, None], scores + bias[None, None, None], NEG_INF)
    m = jnp.max(logits, axis=-1, keepdims=True)
    p = jnp.exp(logits - m)
    denom = jnp.sum(p, axis=-1, keepdims=True)
    lse = (m + jnp.log(denom))[..., 0]
    o = jnp.einsum('bnrhqk,bnkrhd->bnqrhd', (p / denom).astype(v.dtype), vw,
                   preferred_element_type=jnp.float32)
    out = o.reshape(b, s_pad, h, dh)[:, :s]
    lse = lse.transpose(0, 1, 4, 2, 3).reshape(b, s_pad, h)[:, :s]
    return out, lse


def forgetting_attention(q, k, v, log_f):
    b, s, h, dh = q.shape
    c = jnp.cumsum(log_f, axis=1)
    s_pad = -(-s // FOX_BLOCK) * FOX_BLOCK
    nblk = s_pad // FOX_BLOCK
    qb = jnp.pad(q, ((0, 0), (0, s_pad - s), (0, 0), (0, 0)))
    qb = qb.reshape(b, nblk, FOX_BLOCK, h, dh).transpose(1, 0, 2, 3, 4)
    cqb = jnp.pad(c, ((0, 0), (0, s_pad - s), (0, 0)))
    cqb = cqb.reshape(b, nblk, FOX_BLOCK, h).transpose(1, 0, 3, 2)
    ck = c.transpose(0, 2, 1)
    starts = jnp.arange(nblk) * FOX_BLOCK
    kpos = jnp.arange(s)

    def block(args):
        q_blk, c_blk, start = args
        sc = jnp.einsum('bqhd,bkhd->bhqk', q_blk, k,
                        preferred_element_type=jnp.float32) * ATTN_SCALE
        decay = c_blk[..., None] - ck[:, :, None, :]
        qpos = start + jnp.arange(FOX_BLOCK)
        mask = kpos[None, :] <= qpos[:, None]
        p = jax.nn.softmax(jnp.where(mask, sc + decay, NEG_INF), axis=-1)
        return jnp.einsum('bhqk,bkhd->bqhd', p.astype(v.dtype), v,
                          preferred_element_type=jnp.float32)

    out = lax.map(block, (qb, cqb, starts))
    return out.transpose(1, 0, 2, 3, 4).reshape(b, s_pad, h, dh)[:, :s]


def setup_inputs(seed: int = 0) -> dict:
    key = jax.random.key(seed)
    ks = jax.random.split(key, 13)
    f32 = jnp.float32

    def dense(k, shape, fan_in, gain=1.0):
        return jax.random.normal(k, shape, f32) * (gain * fan_in ** -0.5)

    return {
        'x': jax.random.normal(ks[0], (BATCH, SEQ, D_MODEL), f32),
        'norm_mix_g': 1.0 + 0.01 * jax.random.normal(ks[1], (DEPTH, D_MODEL), f32),
        'w_in': dense(ks[2], (DEPTH, D_MODEL, IN_COLS), D_MODEL),
        'b_fgt': jax.random.uniform(ks[3], (DEPTH, N_FOX_HEADS), f32, 1.0, 4.0),
        'b_gate': 0.01 * jax.random.normal(ks[4], (DEPTH, N_BRANCHES * D_MODEL), f32),
        'w_dil_out': dense(ks[5], (DEPTH, DIL_OUT_WIDTH, D_MODEL), DIL_OUT_WIDTH),
        'w_fox_out': dense(ks[6], (DEPTH, FOX_WIDTH, D_MODEL), FOX_WIDTH),
        'w_out': dense(ks[7], (DEPTH, D_MODEL, D_MODEL), D_MODEL),
        'norm_ffn_g': 1.0 + 0.01 * jax.random.normal(ks[8], (DEPTH, D_MODEL), f32),
        'w_ffn_in': dense(ks[9], (DEPTH, D_MODEL, 2 * D_FF), D_MODEL),
        'w_ffn_down': dense(ks[10], (DEPTH, D_FF, D_MODEL), D_FF),
        'norm_final_g': 1.0 + 0.01 * jax.random.normal(ks[11], (D_MODEL,), f32),
    }


def reference(x, norm_mix_g, w_in, b_fgt, b_gate, w_dil_out, w_fox_out, w_out,
              norm_ffn_g, w_ffn_in, w_ffn_down, norm_final_g):
    b, s, _ = x.shape
    slopes = alibi_slopes(N_DIL_HEADS).reshape(N_DIL_GROUPS, HEADS_PER_DIL_GROUP)
    o1 = DIL_WIDTH
    o2 = 3 * DIL_WIDTH
    o3 = o2 + 3 * FOX_WIDTH
    o4 = o3 + N_FOX_HEADS
    for layer in range(DEPTH):
        h = rms_norm(x, norm_mix_g[layer])
        proj = h @ w_in[layer]
        dil = proj[..., :o2].reshape(b, s, 3, N_DIL_GROUPS, HEADS_PER_DIL_GROUP, HEAD_DIM)
        fox = proj[..., o2:o3].reshape(b, s, 3, N_FOX_HEADS, HEAD_DIM)
        f_logit = proj[..., o3:o4].astype(jnp.float32) + b_fgt[layer]
        gates = jax.nn.sigmoid(proj[..., o4:].astype(jnp.float32) + b_gate[layer])
        g_a, g_b = gates[..., :D_MODEL], gates[..., D_MODEL:]

        outs, lses = [], []
        for g, (window, dilation) in enumerate(DIL_PAIRS):
            o_g, lse_g = dilated_window_attention(dil[:, :, 0, g], dil[:, :, 1, g],
                                                  dil[:, :, 2, g], window, dilation, slopes[g])
            outs.append(o_g)
            lses.append(lse_g)
        alpha = jax.nn.softmax(jnp.stack(lses, axis=0), axis=0)
        o_a = jnp.sum(alpha[..., None] * jnp.stack(outs, axis=0), axis=0)
        y_a = o_a.reshape(b, s, DIL_OUT_WIDTH).astype(x.dtype) @ w_dil_out[layer]

        o_b = forgetting_attention(fox[:, :, 0], fox[:, :, 1], fox[:, :, 2],
                                   jax.nn.log_sigmoid(f_logit))
        y_b = o_b.reshape(b, s, FOX_WIDTH).astype(x.dtype) @ w_fox_out[layer]

        merged = (g_a * y_a + g_b * y_b).astype(x.dtype)
        x = x + merged @ w_out[layer]

        h2 = rms_norm(x, norm_ffn_g[layer])
        gu = h2 @ w_ffn_in[layer]
        x = x + (jax.nn.silu(gu[..., :D_FF]) * gu[..., D_FF:]) @ w_ffn_down[layer]
    return rms_norm(x, norm_final_g)
```

```python
import math
from contextlib import ExitStack

import numpy as np
import ml_dtypes
import concourse.bass as bass
import concourse.mybir as mybir
from concourse.bass_utils import run_bass_kernel_spmd

F32 = mybir.dt.float32
BF16 = mybir.dt.bfloat16
AF = mybir.ActivationFunctionType
ALU = mybir.AluOpType

T = 4096
D = 1024
NCH = 8
DFF = 2816
NJ = 22
IN_COLS = 5896
EPS = 1e-6
NEG = -60000.0
DILS = (1, 4, 16)


class Ev:
    __slots__ = ("sem", "val")

    def __init__(self, sem, val):
        self.sem = sem
        self.val = val


class _Rec:
    def __getattr__(self, name):
        def f(*a, **k):
            self.__dict__["call"] = (name, a, k)
            return self
        return f


class TR:
    ENGS = ("pe", "act", "dve", "pool", "sp")

    def __init__(self, nc, stack):
        self.nc = nc
        self.ops = {e: [] for e in self.ENGS}
        self.sem = {}
        self.cnt = {}
        self.seen = {e: {} for e in self.ENGS}
        self.stack = stack
        for e in self.ENGS:
            self.sem[e] = stack.enter_context(nc.semaphore("s_" + e))
            self.cnt[e] = 0
        self.pe_pending = []
        self.lastw = {}
        self.readers = {}
        self.slots = []

    def new_slot(self, name):
        key = "slot_" + name
        self.sem[key] = self.stack.enter_context(self.nc.semaphore("d_" + name))
        self.cnt[key] = 0
        self.slots.append(key)
        return key

    def _deps(self, reads, writes):
        deps = []
        for r in reads:
            w = self.lastw.get(r)
            if w is not None:
                deps.append(w)
        for w_ in writes:
            w = self.lastw.get(w_)
            if w is not None:
                deps.append(w)
            deps.extend(self.readers.get(w_, ()))
        return deps

    def _emit_waits(self, eng, deps):
        best = {}
        for ev in deps:
            if ev.val is None:
                if eng == "pe":
                    continue
                raise RuntimeError("unflushed PE group dependency")
            if ev.sem == "pe" and eng == "pe":
                continue
            if ev.val > best.get(ev.sem, 0):
                best[ev.sem] = ev.val
        for s, v in best.items():
            if v > self.seen[eng].get(s, 0):
                self.seen[eng][s] = v
                semh = self.sem[s]
                self.ops[eng].append(lambda e, semh=semh, v=v: e.wait_ge(semh, v))

    def _record(self, ev, reads, writes):
        for w_ in writes:
            self.lastw[w_] = ev
            self.readers[w_] = []
        for r in reads:
            self.readers.setdefault(r, []).append(ev)

    def op(self, eng, fn0, reads=(), writes=(), inc=True):
        rec = _Rec()
        fn0(rec)
        name_, a_, k_ = rec.call
        fn = lambda e, name_=name_, a_=a_, k_=k_: getattr(e, name_)(*a_, **k_)
        deps = self._deps(reads, writes)
        self._emit_waits(eng, deps)
        if eng == "pe" and not inc:
            ev = Ev("pe", None)
            self.pe_pending.append(ev)
            self.ops[eng].append(lambda e, fn=fn: fn(e))
        else:
            self.cnt[eng] += 1
            v = self.cnt[eng]
            ev = Ev(eng, v)
            semh = self.sem[eng]
            self.ops[eng].append(lambda e, fn=fn, semh=semh: fn(e).then_inc(semh, 1))
            if eng == "pe":
                for p in self.pe_pending:
                    p.val = v
                self.pe_pending = []
        self._record(ev, reads, writes)
        return ev

    def dma(self, eng, slot, out, in_, reads=(), writes=(), **kw):
        deps = self._deps(reads, writes)
        self._emit_waits(eng, deps)
        self.cnt[slot] += 16
        ev = Ev(slot, self.cnt[slot])
        semh = self.sem[slot]
        self.ops[eng].append(lambda e, out=out, in_=in_, semh=semh, kw=kw: e.dma_start(out=out, in_=in_, **kw).then_inc(semh, 16))
        self._record(ev, reads, writes)
        return ev

    def barrier(self):
        assert not self.pe_pending
        evs = [Ev(k, v) for k, v in self.cnt.items() if v > 0]
        for e in self.ENGS:
            self._emit_waits(e, evs)

    def emit(self):
        nc = self.nc
        ops = self.ops
        with nc.Block() as block:
            @block.tensor
            def _(e):
                for f in ops["pe"]:
                    f(e)

            @block.scalar
            def _(e):
                for f in ops["act"]:
                    f(e)

            @block.vector
            def _(e):
                for f in ops["dve"]:
                    f(e)

            @block.gpsimd
            def _(e):
                for f in ops["pool"]:
                    f(e)

            @block.sync
            def _(e):
                for f in ops["sp"]:
                    f(e)


def build_program():
    nc = bass.Bass("TRN2", target_bir_lowering=False)

    def dram(name, shape, dtype=F32, kind="ExternalInput"):
        return nc.dram_tensor(name, shape, dtype, kind=kind).ap()

    x_d = dram("x", [T, D])
    gmix_d = dram("norm_mix_g", [1, D])
    w_in_d = dram("w_in", [D, IN_COLS])
    bfgt_d = dram("b_fgt", [8, 1])
    bgate_d = dram("b_gate", [16, 128])
    wdo_d = dram("w_dil_out", [256, D])
    wfo_d = dram("w_fox_out", [512, D])
    wo_d = dram("w_out", [D, D])
    gffn_d = dram("norm_ffn_g", [1, D])
    wfi_d = dram("w_ffn_in", [D, 2 * DFF])
    wfd_d = dram("w_ffn_down", [DFF, D])
    gfin_d = dram("norm_final_g", [1, D])
    ident_d = dram("c_ident", [128, 128], BF16)
    cmask_d = dram("c_cmask", [128, 128], BF16)
    dilb_d = dram("c_dilb", [12, 128, 2, 256], BF16)
    out_d = dram("out", [T, D], F32, kind="ExternalOutput")

    KB = 1024
    with ExitStack() as st:
        tr = TR(nc, st)
        AR = st.enter_context(nc.sbuf_tensor("arena", [128, 207 * 512], BF16))
        PSALL = st.enter_context(nc.psum_tensor("psall", [128, 4096], F32))
        PS = [PSALL[:, i * 512:(i + 1) * 512] for i in range(8)]

        def view(off, shape, dtype=BF16, p0=0, p1=128):
            n = 1
            for s in shape:
                n *= s
            esz = 4 if dtype == F32 else 2
            assert off % 4 == 0
            a = AR[p0:p1, off // 2: off // 2 + n * esz // 2]
            if dtype == F32:
                a = a.bitcast(F32)
            if len(shape) == 2:
                a = a.rearrange("p (a b) -> p a b", a=shape[0])
            elif len(shape) == 3:
                a = a.rearrange("p (a b c) -> p a b c", a=shape[0], b=shape[1])
            return a

        H4 = view(0, [8, 8, 512])
        OA = view(64 * KB, [2, T])
        C0 = 80 * KB
        ident = view(C0, [128])
        cmask = view(C0 + 256, [128])
        ones32 = view(C0 + 512, [64], F32)
        epsb = view(C0 + 768, [1], F32)
        ss = view(C0 + 1024, [32], F32)
        sd = view(C0 + 1152, [32], F32)
        rstd = view(C0 + 1280, [32], F32)
        bgate = view(C0 + 1408, [16], F32)
        ss2 = view(C0 + 1536, [32], F32)
        sd2 = view(C0 + 1664, [32], F32)
        rstd2 = view(C0 + 1792, [32], F32)
        ss3 = view(C0 + 1920, [32], F32)
        sd3 = view(C0 + 2048, [32], F32)
        rstd3 = view(C0 + 2176, [32], F32)
        nbf = view(C0 + 2304, [1], F32)
        onesb = view(C0 + 2560, [512], BF16)
        GA = view(84 * KB, [D], F32)
        GB = view(88 * KB, [D], F32)
        OB = view(92 * KB, [4, T])
        L0 = 124 * KB

        psb = [p.bitcast(BF16) for p in PS]

        tr.dma("sp", tr.new_slot("c1"), ident, ident_d, writes=["ident"])
        tr.dma("sp", tr.new_slot("c2"), cmask, cmask_d, writes=["cmask"])
        tr.op("dve", lambda e: e.memset(ones32, 1.0), writes=["ones32"])
        tr.op("dve", lambda e: e.memset(epsb, EPS), writes=["epsb"])
        tr.op("dve", lambda e: e.memset(onesb, 1.0), writes=["onesb"])
        for s_ in (ss, ss2, ss3):
            tr.op("dve", lambda e, s_=s_: e.memset(s_, 0.0), writes=[("ssinit", id(s_))])
        s_g = tr.new_slot("gain")
        tr.dma("sp", s_g, GA, gmix_d.partition_broadcast(128), writes=["GA"])
        tr.barrier()

        xt = [view(L0 + i * 4 * KB, [D], F32) for i in range(8)]
        hn = [view(L0 + 32 * KB + i * 2 * KB, [D], BF16) for i in range(2)]
        junk = view(L0 + 36 * KB, [D], BF16)
        s_x = [tr.new_slot("x%d" % i) for i in range(8)]

        def rms_stats(c, srcs, ssv, sdv, rstdv, tagp):
            for i in range(4):
                t = 4 * c + i
                tr.op("act", lambda e, i=i, t=t: e.activation(out=junk, in_=srcs[(c % 2) * 4 + i], func=AF.Square,
                                                               accum_out=ssv[:, t:t + 1]),
                      reads=[(tagp, "src", (c % 2) * 4 + i)], writes=["junk", (tagp, "ss", c)])
            tr.op("act", lambda e: e.activation(out=sdv[:, 4 * c:4 * c + 4], in_=ssv[:, 4 * c:4 * c + 4], func=AF.Sqrt,
                                                bias=epsb, scale=1.0 / D),
                  reads=[(tagp, "ss", c), "epsb"], writes=[(tagp, "sd", c)])
            tr.op("dve", lambda e: e.reciprocal(out=rstdv[:, 4 * c:4 * c + 4], in_=sdv[:, 4 * c:4 * c + 4]),
                  reads=[(tagp, "sd", c)], writes=[(tagp, "rstd", c)])

        def p0_load_stats(c):
            for i in range(4):
                t = 4 * c + i
                xi = (c % 2) * 4 + i
                tr.dma(("sp", "pool", "act", "pool")[i], s_x[xi], xt[xi], x_d[t * 128:(t + 1) * 128, :], writes=[("p0", "src", xi)])
            rms_stats(c, xt, ss, sd, rstd, "p0")

        s_w = [tr.new_slot("w%d" % i) for i in range(4)]
        QT0 = view(156 * KB + 17024, [T])
        KT0 = view(156 * KB + 17024 + 8 * KB, [T])
        WV0 = view(156 * KB + 17024 + 16 * KB + 8448, [8, 256])
        tr.dma("pool", s_w[2], WV0, w_in_d[:, 1536:1792].rearrange("(k p) c -> p k c", p=128), writes=["WV"])

        def p0_N(c, i):
            t = 4 * c + i
            b = t % 2
            xi = (c % 2) * 4 + i
            tr.op("dve", lambda e: e.scalar_tensor_tensor(
                out=hn[b], in0=xt[xi], scalar=rstd[:, t:t + 1], in1=GA, op0=ALU.mult, op1=ALU.mult),
                reads=[("p0", "src", xi), ("p0", "rstd", c), "GA"], writes=[("hn", b)])
            pb = t % 4
            for kc in range(8):
                tr.op("pe", lambda e: e.transpose(
                    out=psb[pb][:, kc * 128:(kc + 1) * 128], in_=hn[b][:, kc * 128:(kc + 1) * 128], identity=ident),
                    reads=[("hn", b), "ident"], writes=[("ps", pb)], inc=(kc == 7))

        def p0_E(c, i):
            pb = (4 * c + i) % 4
            tr.op("dve", lambda e: e.tensor_copy(
                      out=H4[:, c, :, i * 128:(i + 1) * 128],
                      in_=psb[pb].rearrange("p (k t) -> p k t", k=8)),
                  reads=[("ps", pb)], writes=[("H", c)])
            if i == 3:
                for ft in range(2):
                    vkey = "QT" if ft == 0 else "KT"
                    xkey = "QTx" if ft == 0 else "KTx"
                    vt = QT0 if ft == 0 else KT0
                    pbv = 4 + ft
                    for kc in range(8):
                        tr.op("pe", lambda e: e.matmul(PS[pbv][:, :], lhsT=WV0[:, kc, ft * 128:(ft + 1) * 128], rhs=H4[:, c, kc, :],
                                                       start=(kc == 0), stop=(kc == 7)),
                              reads=["WV", ("H", c)], writes=[("ps", pbv)], inc=(kc == 7))
                    tr.op("act", lambda e: e.activation(out=vt[:, c * 512:(c + 1) * 512], in_=PS[pbv][:, :], func=AF.Copy),
                          reads=[("ps", pbv)], writes=[vkey, xkey])

        seq = [(c, i) for c in range(NCH) for i in range(4)]
        p0_load_stats(0)
        p0_load_stats(1)
        p0_N(0, 0)
        for k in range(len(seq)):
            if k + 1 < len(seq):
                c1, i1 = seq[k + 1]
                if i1 == 0 and c1 + 1 < NCH:
                    p0_load_stats(c1 + 1)
                p0_N(c1, i1)
            p0_E(*seq[k])

        tr.barrier()
        tr.dma("sp", tr.new_slot("bg"), bgate, bgate_d.rearrange("c p -> p c"), writes=["bgate"], allow_slow_non_contiguous=True)

        ACC = view(92 * KB, [4, T], F32, 0, 128)
        VA = view(156 * KB, [32, 4, 65])
        QT = view(156 * KB + 16640 + 384, [T])
        KT = view(156 * KB + 16640 + 384 + 8 * KB, [T])
        QTF, KTF = QT, KT
        WB = 156 * KB + 16640 + 384 + 16 * KB
        WQ = view(WB, [8, 264])
        WK = view(WB + 4224, [8, 264])
        WV = view(WB + 8448, [8, 256])
        BTS = [view(WB + 12544 + i * 1024, [2, 256]) for i in range(2)]
        PT = [view(WB + 14592 + i * 512, [256]) for i in range(4)]
        assert WB + 14592 + 4 * 512 <= 207 * KB
        KTB = view(64 * KB, [T])
        s_bts = [tr.new_slot("bts%d" % i) for i in range(2)]
        RD = view(WB, [512], F32)
        BCS = view(WB + 2 * KB, [512], F32)
        TMPO = [view(WB + 4 * KB + i * KB, [512], BF16) for i in range(2)]

        s_bt = tr.new_slot("bt")
        s_mv = [tr.new_slot("mv%d" % i) for i in range(2)]

        tr.op("dve", lambda e: e.memset(VA[:, :, :, 64:65], 1.0), writes=["VAones"])
        tr.op("dve", lambda e: e.memset(KTB[0:64, :], 0.0), writes=["KTB"])
        tr.op("dve", lambda e: e.memset(WQ[:, :, 256:264], 0.0), writes=["WQpad"])
        tr.op("dve", lambda e: e.memset(WK[:, :, 256:264], 0.0), writes=["WKpad"])

        def hcols(kc, start, stride, count):
            last = start + (count - 1) * stride
            c0, c1 = start // 512, last // 512
            if c0 == c1:
                o = start % 512
                return H4[:, c0, kc, o:o + (count - 1) * stride + 1:stride]
            assert stride == 16 and count == 128 and start % 2048 < 16
            return H4[:, c0:c0 + 4, kc, (start % 512)::16]

        def wcols(col0, n):
            return w_in_d[:, col0:col0 + n].rearrange("(k p) c -> p k c", p=128)

        def qk_proj(wt, wtag, wc0, dst, dtag, pbanks):
            for c in range(NCH):
                pb = pbanks[c % len(pbanks)]
                for kc in range(8):
                    tr.op("pe", lambda e, c=c, kc=kc, pb=pb: e.matmul(
                        PS[pb][0:70, :], lhsT=wt[:, kc, wc0:wc0 + 70], rhs=H4[:, c, kc, :],
                        start=(kc == 0), stop=(kc == 7)),
                        reads=[wtag, ("H", c)], writes=[("ps", pb)], inc=(kc == 7))
                if c % 2 == 0:
                    tr.op("dve", lambda e, c=c, pb=pb: e.tensor_copy(out=dst[0:64, c * 512:(c + 1) * 512], in_=PS[pb][0:64, :]),
                          reads=[("ps", pb)], writes=[dtag])
                else:
                    tr.op("act", lambda e, c=c, pb=pb: e.activation(out=dst[0:64, c * 512:(c + 1) * 512], in_=PS[pb][0:64, :], func=AF.Copy),
                          reads=[("ps", pb)], writes=[dtag])

        def normalize_out(src_num, src_den_row, rtag, OX, h, c, pbank):
            tr.op("dve", lambda e: e.reciprocal(out=RD[64:65, :], in_=src_den_row), reads=[rtag], writes=["RD"])
            tr.op("pe", lambda e: e.matmul(PS[pbank][0:64, :], lhsT=ones32[64:65, 0:64], rhs=RD[64:65, :], start=True, stop=True),
                  reads=["RD", "ones32"], writes=[("ps", pbank)])
            tr.op("act", lambda e: e.activation(out=BCS[0:64, :], in_=PS[pbank][0:64, :], func=AF.Copy),
                  reads=[("ps", pbank)], writes=["BCS"])
            if h % 2 == 0:
                tr.op("dve", lambda e: e.tensor_tensor(out=OX[0:64, h // 2, c * 512:(c + 1) * 512], in0=src_num, in1=BCS[0:64, :], op=ALU.mult),
                      reads=[rtag, "BCS"], writes=[("OX", id(OX), h, c)])
            else:
                b = c % 2
                tr.op("dve", lambda e: e.tensor_tensor(out=TMPO[b][0:64, :], in0=src_num, in1=BCS[0:64, :], op=ALU.mult),
                      reads=[rtag, "BCS"], writes=[("TMPO", b)])
                tr.dma("sp", s_mv[b], OX[64:128, h // 2, c * 512:(c + 1) * 512], TMPO[b][0:64, :],
                       reads=[("TMPO", b)], writes=[("OX", id(OX), h, c)])

        for g in range(3):
            d = DILS[g]
            tr.dma("pool", s_w[0], WQ[:, :, 0:256], wcols(g * 256, 256), writes=["WQ"])
            tr.dma("pool", s_w[1], WK[:, :, 0:256], wcols(768 + g * 256, 256), writes=["WK"])
            if g > 0:
                tr.dma("pool", s_w[2], WV, wcols(1536 + g * 256, 256), writes=["WV"])
            VT = (QTF, KTF)
            for ft in (range(2) if g > 0 else ()):
                vkey = "QT" if ft == 0 else "KT"
                xkey = "QTx" if ft == 0 else "KTx"
                for c in range(NCH):
                    pb = 4 + (c % 2)
                    for kc in range(8):
                        tr.op("pe", lambda e: e.matmul(PS[pb][:, :], lhsT=WV[:, kc, ft * 128:(ft + 1) * 128], rhs=H4[:, c, kc, :],
                                                       start=(kc == 0), stop=(kc == 7)),
                              reads=["WV", ("H", c)], writes=[("ps", pb)], inc=(kc == 7))
                    if c % 2 == 0:
                        tr.op("dve", lambda e: e.tensor_copy(out=VT[ft][:, c * 512:(c + 1) * 512], in_=PS[pb][:, :]),
                              reads=[("ps", pb)], writes=[vkey, xkey])
                    else:
                        tr.op("act", lambda e: e.activation(out=VT[ft][:, c * 512:(c + 1) * 512], in_=PS[pb][:, :], func=AF.Copy),
                              reads=[("ps", pb)], writes=[vkey, xkey])
            for j in range(32):
                n, r = j // d, j % d
                pb = 4 + (j % 2)
                qs = n * 128 * d + r
                for ft in range(2):
                    vkey = "QT" if ft == 0 else "KT"
                    tr.op("pe", lambda e: e.transpose(out=psb[pb][:, ft * 128:(ft + 1) * 128], in_=VT[ft][:, qs:qs + 127 * d + 1:d], identity=ident),
                          reads=[vkey, "ident"], writes=[("ps", pb)], inc=(ft == 1))
                if j % 2:
                    tr.op("dve", lambda e: e.tensor_copy(out=VA[:, j, :, 0:64], in_=psb[pb][:, 0:256].rearrange("p (h d) -> p h d", h=4)),
                          reads=[("ps", pb)], writes=["VA"])
                else:
                    tr.op("act", lambda e: e.activation(out=VA[:, j, :, 0:64], in_=psb[pb][:, 0:256].rearrange("p (h d) -> p h d", h=4), func=AF.Copy),
                          reads=[("ps", pb)], writes=["VA"])
            tr.op("dve", lambda e: e.memset(KT[64:128, :], 0.0), reads=[], writes=["KTx", "KT"])
            for h in range(4):
                BT = BTS[h % 2]
                tr.dma("sp", s_bts[h % 2], BT, dilb_d[g * 4 + h], writes=[("BT", h % 2)])
                if h % 2 == 0:
                    p = h // 2
                    for c in range(NCH):
                        pbq, pbk = 4, 5
                        for kc in range(8):
                            tr.op("pe", lambda e: e.matmul(PS[pbq][:, :], lhsT=WQ[:, kc, p * 128:(p + 1) * 128], rhs=H4[:, c, kc, :],
                                                           start=(kc == 0), stop=(kc == 7)),
                                  reads=["WQ", ("H", c)], writes=[("ps", pbq)], inc=(kc == 7))
                        tr.op("dve", lambda e: e.tensor_copy(out=QT[:, c * 512:(c + 1) * 512], in_=PS[pbq][:, :]),
                              reads=[("ps", pbq)], writes=["QT"])
                        for kc in range(8):
                            tr.op("pe", lambda e: e.matmul(PS[pbk][:, :], lhsT=WK[:, kc, p * 128:(p + 1) * 128], rhs=H4[:, c, kc, :],
                                                           start=(kc == 0), stop=(kc == 7)),
                                  reads=["WK", ("H", c)], writes=[("ps", pbk)], inc=(kc == 7))
                        tr.op("act", lambda e: e.activation(out=KT[0:64, c * 512:(c + 1) * 512], in_=PS[pbk][0:64, :], func=AF.Copy),
                              reads=[("ps", pbk)], writes=["KT"])
                        tr.op("dve", lambda e: e.tensor_copy(out=KTB[64:128, c * 512:(c + 1) * 512], in_=PS[pbk][64:128, :]),
                              reads=[("ps", pbk)], writes=["KTB"])
                KTS = KT if h % 2 == 0 else KTB
                ktkey = "KT" if h % 2 == 0 else "KTB"
                nb = 32 // d
                units = [(r, n) for r in range(d) for n in range(nb)]

                def emit_S(u):
                    r, n = units[u]
                    W = 256 if n + 1 < nb else 128
                    sb_ = u % 3
                    qs = n * 128 * d + r
                    rhs = QT[0:128, qs:qs + (W - 1) * d + 1:d]
                    lhsT = KTS[0:128, qs:qs + 127 * d + 1:d]
                    o_ps = PS[sb_][:, 0:W]
                    tr.op("pe", lambda e: e.matmul(o_ps, lhsT=lhsT, rhs=rhs, start=True, stop=False),
                          reads=["QT", ktkey, "KTx"], writes=[("ps", sb_)], inc=False)
                    tr.op("pe", lambda e: e.matmul(o_ps, lhsT=ident, rhs=BT[:, 0, 0:W], start=False, stop=False),
                          reads=[("BT", h % 2), "ident"], writes=[("ps", sb_)], inc=False)
                    tr.op("pe", lambda e: e.matmul(o_ps, lhsT=ident, rhs=BT[:, 1, 0:W], start=False, stop=True),
                          reads=[("BT", h % 2), "ident"], writes=[("ps", sb_)], inc=True)

                def emit_E(u):
                    r, n = units[u]
                    W = 256 if n + 1 < nb else 128
                    sb_ = u % 3
                    tr.op("act", lambda e: e.activation(out=PT[u % 4][:, 0:W], in_=PS[sb_][:, 0:W], func=AF.Exp, scale=0.125),
                          reads=[("ps", sb_)], writes=[("PT", u % 4)])

                def emit_PV(u):
                    r, n = units[u]
                    ob = 6 + ((u // 4) % 2)
                    jj = u % 4
                    o_ps = PS[ob][0:65, jj * 128:(jj + 1) * 128]
                    if n >= 1:
                        tr.op("pe", lambda e: e.matmul(o_ps, lhsT=VA[:, (n - 1) * d + r, h, :], rhs=PT[(u - 1) % 4][:, 128:256], start=True, stop=False),
                              reads=["VA", "VAones", ("PT", (u - 1) % 4)], writes=[("ps", ob)], inc=False)
                    tr.op("pe", lambda e: e.matmul(o_ps, lhsT=VA[:, n * d + r, h, :], rhs=PT[u % 4][:, 0:128], start=(n == 0), stop=True),
                          reads=["VA", "VAones", ("PT", u % 4)], writes=[("ps", ob)], inc=True)
                    if jj == 3:
                        accv = ACC[0:65, h, :]
                        if g == 0:
                            n0 = n - 3
                            dst = accv[:, n0 * 128:n0 * 128 + 512]
                            tr.op("dve", lambda e: e.tensor_copy(out=dst, in_=PS[ob][0:65, :]),
                                  reads=[("ps", ob)], writes=[("ACC", h)])
                        else:
                            if g == 1:
                                n0 = n - 3
                                dst = accv.rearrange("p (n i r) -> p n i r", n=8, r=4)[:, n0:n0 + 4, :, r]
                                src = PS[ob][0:65, :].rearrange("p (a i) -> p a i", a=4)
                            else:
                                r0 = r - 1
                                dst = accv.rearrange("p (n i r) -> p r n i", n=2, r=16)[:, r0:r0 + 2, :, :]
                                src = PS[ob][0:65, :].rearrange("p (a b i) -> p a b i", a=2, b=2)
                            tr.op("dve", lambda e: e.tensor_tensor(out=dst, in0=src, in1=dst, op=ALU.add),
                                  reads=[("ps", ob)], writes=[("ACC", h)])

                LA = 2
                for u in range(LA):
                    emit_S(u)
                for u in range(32):
                    if u + LA < 32:
                        emit_S(u + LA)
                    emit_E(u)
                    emit_PV(u)
        tr.barrier()
        DN0 = 200064
        LNS = [view(DN0 + i * 2 * KB, [512], F32) for i in range(2)]
        BC2 = [view(DN0 + 4 * KB + i * 2 * KB, [512], F32) for i in range(2)]
        TM2 = [view(DN0 + 8 * KB + i * KB, [512], BF16) for i in range(2)]
        assert DN0 + 10 * KB <= 207 * KB

        def dn_step(stp):
            h, c = stp // NCH, stp % NCH
            b = stp % 2
            cs = slice(c * 512, (c + 1) * 512)
            pbank = 6 + b
            tr.op("pe", lambda e: e.matmul(PS[pbank][0:64, :], lhsT=ones32[64:65, 0:64], rhs=ACC[64:65, h, cs], start=True, stop=True),
                  reads=["ones32"], writes=[("ps", pbank)])
            tr.op("act", lambda e: e.activation(out=LNS[b][0:64, :], in_=PS[pbank][0:64, :], func=AF.Ln),
                  reads=[("ps", pbank)], writes=[("LNS", b)])
            tr.op("act", lambda e: e.activation(out=BC2[b][0:64, :], in_=LNS[b][0:64, :], func=AF.Exp, scale=-1.0),
                  reads=[("LNS", b)], writes=[("BC2", b)])
            if h % 2 == 0:
                tr.op("dve", lambda e: e.tensor_tensor(out=OA[0:64, h // 2, cs], in0=ACC[0:64, h, cs], in1=BC2[b][0:64, :], op=ALU.mult),
                      reads=[("BC2", b)], writes=[("OA", h, c)])
            else:
                tr.op("dve", lambda e: e.tensor_tensor(out=TM2[b][0:64, :], in0=ACC[0:64, h, cs], in1=BC2[b][0:64, :], op=ALU.mult),
                      reads=[("BC2", b)], writes=[("TM2", b)])
                tr.dma("sp", s_mv[b], OA[64:128, h // 2, cs], TM2[b][0:64, :], reads=[("TM2", b)], writes=[("OA", h, c)])

        VB = view(L0, [32, 4, 65])
        QTS = [view(L0 + 16640 + 384, [T]), None]
        KTS = [view(L0 + 16640 + 384 + 8 * KB, [T]), None]
        WB = L0 + 16640 + 384 + 16 * KB
        WQP = view(WB, [8, 128])
        WKP = view(WB + 2048, [8, 128])
        WV = view(WB + 4608, [8, 256])
        CQ = view(WB + 8704, [T])
        ZB = WB + 8704 + 8 * KB
        QTS[1] = view(ZB, [T])
        KTS[1] = view(ZB + 8 * KB, [T])
        CT = [view(ZB + i * 2 * KB, [512], F32) for i in range(3)]
        HML = [view(ZB + 6 * KB + i * KB, [512], BF16) for i in range(3)]
        SB_ = ZB + 16 * KB
        WF = view(SB_, [8, 8])
        NBF = view(SB_ + 128, [1], F32)
        BFG = view(SB_ + 192, [1], F32)
        CARRY = view(SB_ + 200, [1], F32)
        PT2 = [view(SB_ + 256 + i * 2 * KB, [1024]) for i in range(3)]
        N0 = SB_ + 256 + 6 * KB
        RD = view(N0, [512], F32)
        BCS = view(N0 + 2 * KB, [512], F32)
        TMPO = [view(N0 + 4 * KB + i * KB, [512], BF16) for i in range(2)]
        assert N0 + 6 * KB <= 207 * KB, (N0 + 6 * KB) / KB

        OBK = (4, 5, 6, 7)
        BCS2 = [RD, BCS]
        s_bc = [tr.new_slot("bc%d" % i) for i in range(2)]
        s_cq = [tr.new_slot("cq%d" % i) for i in range(12)]
        s_hml = [tr.new_slot("hml%d" % i) for i in range(3)]
        s_f = tr.new_slot("wf")
        s_wqh = [tr.new_slot("wqh%d" % i) for i in range(2)]
        s_wkh = [tr.new_slot("wkh%d" % i) for i in range(2)]

        E0, X8, CC = CT[0], CT[1], CT[2]
        tr.dma("pool", s_f, WF, wcols(3840, 8), writes=["WF"])
        tr.dma("sp", tr.new_slot("bf"), BFG[0:8, :], bfgt_d, writes=["BFG"])
        tr.op("dve", lambda e: e.tensor_scalar(out=NBF[0:8, :], in0=BFG[0:8, :], scalar1=-1.0, scalar2=None, op0=ALU.mult),
              reads=["BFG"], writes=["NBF"])
        for c in range(NCH):
            for q_ in range(4):
                dn_step(4 * c + q_)
            pb = 4 + (c % 2)
            cs = slice(c * 512, (c + 1) * 512)
            for kc in range(8):
                tr.op("pe", lambda e: e.matmul(PS[pb][0:8, :], lhsT=WF[:, kc, 0:8], rhs=H4[:, c, kc, :],
                                               start=(kc == 0), stop=(kc == 7)),
                      reads=["WF", ("H", c)], writes=[("ps", pb)], inc=(kc == 7))
            tr.op("act", lambda e: e.activation(out=E0[0:8, :], in_=PS[pb][0:8, :], func=AF.Exp, bias=NBF[0:8, :], scale=-1.0),
                  reads=[("ps", pb), "NBF"], writes=["E0"])
            tr.op("act", lambda e: e.activation(out=E0[0:8, :], in_=E0[0:8, :], func=AF.Ln, bias=ones32[0:8, 0:1], scale=1.0),
                  reads=["E0", "ones32"], writes=["E0"])
            tr.op("dve", lambda e: e.tensor_scalar(out=X8[0:8, :], in0=E0[0:8, :], scalar1=-8.0, scalar2=None, op0=ALU.mult),
                  reads=["E0"], writes=["X8"])
            if c == 0:
                tr.op("dve", lambda e: e.tensor_tensor_scan(out=CC[0:8, :], data0=onesb[0:8, :], data1=X8[0:8, :], initial=0.0,
                                                            op0=ALU.mult, op1=ALU.add),
                      reads=["X8", "onesb"], writes=["CC"])
            else:
                tr.op("dve", lambda e: e.tensor_tensor_scan(out=CC[0:8, :], data0=onesb[0:8, :], data1=X8[0:8, :], initial=CARRY[0:8, :],
                                                            op0=ALU.mult, op1=ALU.add),
                      reads=["X8", "onesb", "CARRY"], writes=["CC"])
            tr.op("dve", lambda e: e.tensor_copy(out=CARRY[0:8, :], in_=CC[0:8, 511:512]), reads=["CC"], writes=["CARRY"])
            tr.op("dve", lambda e: e.tensor_copy(out=HML[0][0:8, :], in_=CC[0:8, :]), reads=["CC"], writes=[("HML", 0)])
            tr.op("dve", lambda e: e.tensor_tensor(out=X8[0:8, :], in0=CC[0:8, :], in1=HML[0][0:8, :], op=ALU.subtract),
                  reads=["CC", ("HML", 0)], writes=["X8"])
            tr.op("dve", lambda e: e.tensor_copy(out=HML[1][0:8, :], in_=X8[0:8, :]), reads=["X8"], writes=[("HML", 1)])
            tr.op("dve", lambda e: e.tensor_tensor(out=E0[0:8, :], in0=X8[0:8, :], in1=HML[1][0:8, :], op=ALU.subtract),
                  reads=["X8", ("HML", 1)], writes=["E0"])
            tr.op("dve", lambda e: e.tensor_copy(out=HML[2][0:8, :], in_=E0[0:8, :]), reads=["E0"], writes=[("HML", 2)])
            for p_ in range(3):
                tr.dma("sp", s_hml[p_], CQ[p_ * 8:p_ * 8 + 8, cs], HML[p_][0:8, :], reads=[("HML", p_)], writes=["CQ"])
        tr.barrier()
        tr.op("dve", lambda e: e.memset(VB[:, :, :, 64:65], 1.0), writes=["VBones"])
        for i in range(2):
            eng_ = "dve"
            tr.op(eng_, lambda e: e.memset(QTS[i][64:128, :], -1.0), writes=[("QTx", i)])
            tr.op(eng_, lambda e: e.memset(KTS[i][64:128, :], 0.0), writes=[("KTx", i)])
            tr.op(eng_, lambda e: e.memset(KTS[i][64:70, :], 1.0), writes=[("KTx", i)])

        STG = [view(N0 + 6 * KB + i * KB, [512]) for i in range(4)]
        assert N0 + 10 * KB <= 207 * KB
        s_stg = [tr.new_slot("stg%d" % i) for i in range(4)]

        def pair_setup(p):
            tr.dma("pool", s_wqh[0], WQP, wcols(2304 + p * 128, 128), writes=["WQP"])
            tr.dma("pool", s_wkh[0], WKP, wcols(2816 + p * 128, 128), writes=["WKP"])
            for wb_ in range(2):
                h = 2 * p + wb_
                for p_ in range(3):
                    tr.dma("pool", s_cq[wb_ * 6 + p_], QTS[wb_][64 + p_:65 + p_, :], CQ[p_ * 8 + h:p_ * 8 + h + 1, :], reads=["CQ"], writes=[("QTx", wb_)])
                    tr.dma("pool", s_cq[wb_ * 6 + 3 + p_], KTS[wb_][67 + p_:68 + p_, :], CQ[p_ * 8 + h:p_ * 8 + h + 1, :], reads=["CQ"], writes=[("KTx", wb_)])

        def pair_proj(p):
            k = 0
            for c in range(NCH):
                cs = slice(c * 512, (c + 1) * 512)
                for qi, (wt, wkey, dsts, dk) in enumerate(((WQP, "WQP", QTS, "QT"), (WKP, "WKP", KTS, "KT"))):
                    pb = 4 + (k % 2)
                    sg = 2 * (c % 2) + qi
                    k += 1
                    for kc in range(8):
                        tr.op("pe", lambda e: e.matmul(PS[pb][:, :], lhsT=wt[:, kc, :], rhs=H4[:, c, kc, :],
                                                       start=(kc == 0), stop=(kc == 7)),
                              reads=[wkey, ("H", c)], writes=[("ps", pb)], inc=(kc == 7))
                    tr.op("act", lambda e: e.activation(out=dsts[0][0:64, cs], in_=PS[pb][0:64, :], func=AF.Copy),
                          reads=[("ps", pb)], writes=[(dk, 0)])
                    tr.op("act", lambda e: e.activation(out=STG[sg][64:128, :], in_=PS[pb][64:128, :], func=AF.Copy),
                          reads=[("ps", pb)], writes=[("STG", sg)])
                    tr.dma("sp", s_stg[sg], dsts[1][0:64, cs], STG[sg][64:128, :], reads=[("STG", sg)], writes=[(dk, 1)])

        WGP = [view(84 * KB, [8, 256]), view(88 * KB, [8, 256])]
        s_gb = tr.new_slot("gb")
        deferred = []
        for h in range(8):
            half, hh = h // 4, h % 4
            qb = h % 2
            QT, KT = QTS[qb], KTS[qb]
            if hh == 0:
                tr.dma("pool", s_w[2], WV, wcols(3328 + half * 256, 256), writes=["WV"])
                for t in range(32):
                    pb = 4 + (t % 2)
                    for kc in range(8):
                        tr.op("pe", lambda e: e.matmul(
                            PS[pb][:, 0:256], lhsT=H4[:, t // 4, kc, (t % 4) * 128:(t % 4 + 1) * 128], rhs=WV[:, kc, :],
                            start=(kc == 0), stop=(kc == 7)),
                            reads=["WV", ("H", t // 4)], writes=[("ps", pb)], inc=(kc == 7))
                    tr.op("act", lambda e: e.activation(out=VB[:, t, :, 0:64], in_=PS[pb][:, 0:256].rearrange("p (h d) -> p h d", h=4), func=AF.Copy),
                          reads=[("ps", pb)], writes=["VB"])
            if qb == 0:
                pair_setup(h // 2)
                pair_proj(h // 2)
                while deferred:
                    deferred.pop(0)()
                if h == 6:
                    tr.dma("pool", s_g, WGP[0], wcols(3848, 256), writes=["GA"])
                    tr.dma("pool", s_gb, WGP[1], wcols(3848 + 1024, 256), writes=["GB"])
            pending = []
            groups = []
            for qc in range(NCH):
                for b in range(0, 4 * qc, 2):
                    groups.append([(qc, b), (qc, b + 1)])
                for b in range(4 * qc, 4 * qc + 4):
                    groups.append([(qc, b)])

            def emit_S(gi):
                bp = 2 * (gi % 2)
                for ti, (qc, b) in enumerate(groups[gi]):
                    r = b - 4 * qc
                    col0 = max(0, r) * 128
                    sb_ = bp + ti
                    tr.op("pe", lambda e: e.matmul(PS[sb_][:, col0:512], lhsT=KT[0:128, b * 128:(b + 1) * 128],
                                                   rhs=QT[0:128, qc * 512 + col0:(qc + 1) * 512], start=True, stop=(r < 0)),
                          reads=[("QT", qb), ("KT", qb), ("QTx", qb), ("KTx", qb)], writes=[("ps", sb_)], inc=(r < 0))
                    if r >= 0:
                        tr.op("pe", lambda e: e.matmul(PS[sb_][:, col0:col0 + 128], lhsT=ident, rhs=cmask, start=False, stop=True),
                              reads=["ident", "cmask"], writes=[("ps", sb_)], inc=True)

            def emit_E(gi):
                bp = 2 * (gi % 2)
                pt = PT2[gi % 3]
                grp = groups[gi]
                if len(grp) == 2:
                    tr.op("act", lambda e: e.activation(out=pt[:, 0:1024], in_=PSALL[:, bp * 512:(bp + 2) * 512], func=AF.Exp, scale=0.125),
                          reads=[("ps", bp), ("ps", bp + 1)], writes=[("PT2", gi % 3)])
                else:
                    qc, b = grp[0]
                    col0 = max(0, b - 4 * qc) * 128
                    tr.op("act", lambda e: e.activation(out=pt[:, col0:512], in_=PS[bp][:, col0:512], func=AF.Exp, scale=0.125),
                          reads=[("ps", bp)], writes=[("PT2", gi % 3)])

            def emit_PV(gi):
                pt = PT2[gi % 3]
                for ti, (qc, b) in enumerate(groups[gi]):
                    col0 = max(0, b - 4 * qc) * 128
                    ob = OBK[(h * 8 + qc) % 4]
                    last = (b == 4 * qc + 3)
                    tr.op("pe", lambda e: e.matmul(PS[ob][0:65, col0:512], lhsT=VB[:, b, hh, :], rhs=pt[:, ti * 512 + col0:(ti + 1) * 512],
                                                   start=(b == 0), stop=last),
                          reads=["VB", "VBones", ("PT2", gi % 3)], writes=[("ps", ob)], inc=(last or ti == len(groups[gi]) - 1))
                    if last:
                        gc = h * 8 + qc
                        b2 = gc % 2
                        cs = slice(qc * 512, (qc + 1) * 512)
                        tr.op("dve", lambda e: e.reciprocal(out=BCS2[b2][64:65, :], in_=PS[ob][64:65, :]),
                              reads=[("ps", ob)], writes=[("BCSr", b2)])

                        def fin_tail(ob=ob, b2=b2, cs=cs, qc=qc, h=h):
                            tr.dma("sp", s_bc[b2], BCS2[b2][0:64, :], BCS2[b2][64:65, :].unsqueeze(1).broadcast_to([1, 64, 512]),
                                   reads=[("BCSr", b2)], writes=[("BCS", b2)])
                            if h % 2 == 0:
                                tr.op("dve", lambda e: e.tensor_tensor(out=OB[0:64, h // 2, cs], in0=PS[ob][0:64, :], in1=BCS2[b2][0:64, :], op=ALU.mult),
                                      reads=[("ps", ob), ("BCS", b2)], writes=[("OB", h, qc)])
                            else:
                                tr.op("dve", lambda e: e.tensor_tensor(out=TMPO[b2][0:64, :], in0=PS[ob][0:64, :], in1=BCS2[b2][0:64, :], op=ALU.mult),
                                      reads=[("ps", ob), ("BCS", b2)], writes=[("TMPO", b2)])
                                tr.dma("sp", s_mv[b2], OB[64:128, h // 2, cs], TMPO[b2][0:64, :],
                                       reads=[("TMPO", b2)], writes=[("OB", h, qc)])
                        if h % 2 == 1 and qc >= NCH - 2 and h < 7:
                            deferred.append(fin_tail)
                        else:
                            fin_tail()

            emit_S(0)
            for gi in range(len(groups)):
                if gi + 1 < len(groups):
                    emit_S(gi + 1)
                emit_E(gi)
                if gi >= 1:
                    emit_PV(gi - 1)
            emit_PV(len(groups) - 1)
        tr.barrier()

        WG = view(L0, [8, 2048])
        WDO = view(L0 + 32 * KB, [2, D])
        WFO = view(L0 + 36 * KB, [4, D])
        WO = view(L0 + 44 * KB, [8, D])
        MT = view(L0 + 60 * KB, [8, 512])
        SG = [view(L0 + 68 * KB + i * 2 * KB, [512], F32) for i in range(2)]
        M12 = [view(L0 + 72 * KB + i * 2 * KB, [512], F32) for i in range(2)]
        sl = [tr.new_slot("p2w%d" % i) for i in range(6)]
        s_wg = [tr.new_slot("wg%d" % i) for i in range(8)]
        s_wdo = [tr.new_slot("wdo%d" % i) for i in range(4)]
        s_wfo = [tr.new_slot("wfo%d" % i) for i in range(4)]
        for k in range(4):
            for gi in range(2):
                if k == 0:
                    continue
                tr.dma("pool", s_wg[gi * 4 + k], WG[:, :, gi * 1024 + k * 256:gi * 1024 + (k + 1) * 256],
                       wcols(3848 + gi * 1024 + k * 256, 256), writes=[("WG", gi, k)])
            tr.dma("pool", s_wdo[k], WDO[:, :, k * 256:(k + 1) * 256], wdo_d[:, k * 256:(k + 1) * 256].rearrange("(k p) c -> p k c", p=128), writes=[("WDO", k)])
            tr.dma("pool", s_wfo[k], WFO[:, :, k * 256:(k + 1) * 256], wfo_d[:, k * 256:(k + 1) * 256].rearrange("(k p) c -> p k c", p=128), writes=[("WFO", k)])
        tr.dma("pool", sl[4], WO, wo_d.rearrange("(k p) c -> p k c", p=128), writes=["WO"])
        Hflat = view(0, [8, 4, D])
        for c in range(NCH):
            cs = slice(c * 512, (c + 1) * 512)
            for dc in range(8):
                for gi in range(2):
                    pb = gi
                    for kc in range(8):
                        wsrc = WGP[gi][:, kc, dc * 128:(dc + 1) * 128] if dc < 2 else WG[:, kc, gi * 1024 + dc * 128: gi * 1024 + (dc + 1) * 128]
                        wkey_ = ("GA" if gi == 0 else "GB") if dc < 2 else ("WG", gi, dc // 2)
                        tr.op("pe", lambda e, kc=kc, gi=gi, dc=dc, c=c, pb=pb: e.matmul(
                            PS[pb][:, :], lhsT=wsrc, rhs=H4[:, c, kc, :],
                            start=(kc == 0), stop=(kc == 7)),
                            reads=[wkey_, ("H", c)], writes=[("ps", pb)], inc=(kc == 7))
                for k in range(2):
                    tr.op("pe", lambda e, k=k, dc=dc, cs=cs: e.matmul(PS[2][:, :], lhsT=WDO[:, k, dc * 128:(dc + 1) * 128], rhs=OA[:, k, cs],
                                                                  start=(k == 0), stop=(k == 1)),
                          reads=[("WDO", dc // 2), "OAall"], writes=[("ps", 2)], inc=(k == 1))
                for k in range(4):
                    tr.op("pe", lambda e, k=k, dc=dc, cs=cs: e.matmul(PS[3][:, :], lhsT=WFO[:, k, dc * 128:(dc + 1) * 128], rhs=OB[:, k, cs],
                                                                  start=(k == 0), stop=(k == 3)),
                          reads=[("WFO", dc // 2), "OBall"], writes=[("ps", 3)], inc=(k == 3))
                for gi in range(2):
                    tr.op("act", lambda e, gi=gi, dc=dc: e.activation(out=SG[gi], in_=PS[gi][:, :], func=AF.Sigmoid,
                                                                     bias=bgate[:, gi * 8 + dc: gi * 8 + dc + 1], scale=1.0),
                          reads=[("ps", gi), "bgate"], writes=[("SG", gi)])
                    tr.op("dve", lambda e, gi=gi: e.tensor_tensor(out=M12[gi], in0=PS[2 + gi][:, :], in1=SG[gi], op=ALU.mult),
                          reads=[("ps", 2 + gi), ("SG", gi)], writes=[("M12", gi)])
                tr.op("dve", lambda e, dc=dc: e.tensor_tensor(out=MT[:, dc, :], in0=M12[0], in1=M12[1], op=ALU.add),
                      reads=[("M12", 0), ("M12", 1)], writes=["MT"])
            for tt in range(4):
                for dh in range(2):
                    pb = 4 + ((tt * 2 + dh) % 2)
                    for kc in range(8):
                        tr.op("pe", lambda e, kc=kc, tt=tt, dh=dh, pb=pb: e.matmul(
                            PS[pb][:, :], lhsT=MT[:, kc, tt * 128:(tt + 1) * 128], rhs=WO[:, kc, dh * 512:(dh + 1) * 512],
                            start=(kc == 0), stop=(kc == 7)),
                            reads=["WO", "MT"], writes=[("ps", pb)], inc=(kc == 7))
                    if dh == 0:
                        tr.op("dve", lambda e, c=c, tt=tt, dh=dh, pb=pb: e.tensor_copy(out=Hflat[:, c, tt, dh * 512:(dh + 1) * 512], in_=PS[pb][:, :]),
                              reads=[("ps", pb)], writes=[("H", c)])
                    else:
                        tr.op("act", lambda e, c=c, tt=tt, dh=dh, pb=pb: e.activation(out=Hflat[:, c, tt, dh * 512:(dh + 1) * 512], in_=PS[pb][:, :], func=AF.Copy),
                              reads=[("ps", pb)], writes=[("H", c)])
        tr.barrier()

        WD = view(92 * KB, [NJ, D])
        X2 = [view(136 * KB + i * 16 * KB, [4, D], F32) for i in range(2)]
        H2T = view(168 * KB, [8, 512])
        AT = view(176 * KB, [NJ, 512])
        H2N = [view(198 * KB + i * 2 * KB, [D]) for i in range(2)] + [view(170 * KB + i * 2 * KB, [D]) for i in range(2)]
        SGF = [view(202 * KB + i * 2 * KB, [512], F32) for i in range(2)]
        WS = [view(64 * KB + i * 8 * KB, [8, 2, 256]) for i in range(2)]
        s_ws = [tr.new_slot("ws%d" % i) for i in range(4)]
        s_x2 = [tr.new_slot("x2%d" % i) for i in range(8)]
        s_o = [tr.new_slot("o%d" % i) for i in range(8)]
        tr.dma("sp", s_g, GA, gffn_d.partition_broadcast(128), writes=["GA"])
        tr.dma("sp", tr.new_slot("gb2"), GB, gfin_d.partition_broadcast(128), writes=["GB"])
        out_evs = []
        JK = view(168 * KB, [D])

        def h2t_of(c):
            return H4[:, c]

        def h2t_keys(c):
            return [("H", c)]

        def prep_load(c):
            xb = X2[c % 2]
            for i in range(4):
                t = 4 * c + i
                sx = s_x2[(c % 2) * 4 + i]
                tr.dma("sp", sx, xb[:, i, :], x_d[t * 128:(t + 1) * 128, :], writes=[("X2", c % 2, i)])

        def prep(c):
            xb = X2[c % 2]
            for i in range(4):
                tr.op("dve", lambda e: e.tensor_tensor(out=xb[:, i, :], in0=xb[:, i, :], in1=Hflat[:, c, i, :], op=ALU.add),
                      reads=[("H", c)], writes=[("X2", c % 2, i)])
            for i in range(4):
                t = 4 * c + i
                tr.op("act", lambda e: e.activation(out=JK, in_=xb[:, i, :], func=AF.Square, accum_out=ss2[:, t:t + 1]),
                      reads=[("X2", c % 2, i)], writes=["JK", ("ss2", c)])
            tr.op("act", lambda e: e.activation(out=sd2[:, 4 * c:4 * c + 4], in_=ss2[:, 4 * c:4 * c + 4], func=AF.Sqrt, bias=epsb, scale=1.0 / D),
                  reads=[("ss2", c), "epsb"], writes=[("sd2", c)])
            tr.op("dve", lambda e: e.reciprocal(out=rstd2[:, 4 * c:4 * c + 4], in_=sd2[:, 4 * c:4 * c + 4]),
                  reads=[("sd2", c)], writes=[("rstd2", c)])
            for i in range(4):
                t = 4 * c + i
                b = i
                tr.op("dve", lambda e: e.scalar_tensor_tensor(
                    out=H2N[b], in0=xb[:, i, :], scalar=rstd2[:, t:t + 1], in1=GA, op0=ALU.mult, op1=ALU.mult),
                    reads=[("X2", c % 2, i), ("rstd2", c), "GA"], writes=[("H2N", b)])

        def prep_back(c):
            ht = h2t_of(c)
            for i in range(4):
                b = i
                pb = (6, 7, 0, 1)[i]
                for kc in range(8):
                    tr.op("pe", lambda e: e.transpose(
                        out=psb[pb][:, kc * 128:(kc + 1) * 128], in_=H2N[b][:, kc * 128:(kc + 1) * 128], identity=ident),
                        reads=[("H2N", b), "ident"], writes=[("ps", pb)], inc=(kc == 7))
                tr.op("dve", lambda e: e.tensor_copy(out=ht[:, :, i * 128:(i + 1) * 128], in_=psb[pb].rearrange("p (k t) -> p k t", k=8)),
                      reads=[("ps", pb)], writes=h2t_keys(c) + [("H2Tc", c)])

        def ffn_in(c, js_list):
            ht = h2t_of(c)
            for js in js_list:
                wsb = js % 2
                tr.dma("pool", s_ws[wsb * 2], WS[wsb][:, :, 0, :], wfi_d[:, js * 256:(js + 1) * 256].rearrange("(k p) c -> p k c", p=128), writes=[("WS", wsb, 0)])
                tr.dma("pool", s_ws[wsb * 2 + 1], WS[wsb][:, :, 1, :], wfi_d[:, DFF + js * 256: DFF + (js + 1) * 256].rearrange("(k p) c -> p k c", p=128), writes=[("WS", wsb, 1)])
                if c == 0:
                    tr.dma("pool", sl[0], WD[:, 2 * js:2 * js + 2, :],
                           wfd_d[2 * js * 128:(2 * js + 2) * 128, :].rearrange("(k p) c -> p k c", p=128), writes=["WD"])
                for jj in range(2):
                    j = 2 * js + jj
                    pg, pu = 2 + 2 * (j % 2), 3 + 2 * (j % 2)
                    for gu, pb in ((0, pg), (1, pu)):
                        for kc in range(8):
                            tr.op("pe", lambda e: e.matmul(
                                PS[pb][:, :], lhsT=WS[wsb][:, kc, gu, jj * 128:(jj + 1) * 128], rhs=ht[:, kc, :],
                                start=(kc == 0), stop=(kc == 7)),
                                reads=[("WS", wsb, gu), ("H2Tc", c)], writes=[("ps", pb)], inc=(kc == 7))
                    sb2 = j % 2
                    tr.op("act", lambda e: e.activation(out=SGF[sb2], in_=PS[pg][:, :], func=AF.Silu),
                          reads=[("ps", pg)], writes=[("SGF", sb2)])
                    tr.op("dve", lambda e: e.tensor_tensor(out=AT[:, j, :], in0=PS[pu][:, :], in1=SGF[sb2], op=ALU.mult),
                          reads=[("ps", pu), ("SGF", sb2)], writes=["AT"])

        def down_final(c):
            xb = X2[c % 2]
            for i in range(4):
                for dh in range(2):
                    pb = (i * 2 + dh) % 2
                    for j in range(NJ):
                        tr.op("pe", lambda e: e.matmul(
                            PS[pb][:, :], lhsT=AT[:, j, i * 128:(i + 1) * 128], rhs=WD[:, j, dh * 512:(dh + 1) * 512],
                            start=(j == 0), stop=(j == NJ - 1)),
                            reads=["AT", "WD"], writes=[("ps", pb)], inc=(j == NJ - 1))
                    tr.op("dve", lambda e: e.tensor_tensor(
                        out=xb[:, i, dh * 512:(dh + 1) * 512], in0=PS[pb][:, :], in1=xb[:, i, dh * 512:(dh + 1) * 512], op=ALU.add),
                        reads=[("ps", pb)], writes=[("X2", c % 2, i)])
            for i in range(4):
                t = 4 * c + i
                tr.op("act", lambda e: e.activation(out=JK, in_=xb[:, i, :], func=AF.Square, accum_out=ss3[:, t:t + 1]),
                      reads=[("X2", c % 2, i)], writes=["JK", ("ss3", c)])
            tr.op("act", lambda e: e.activation(out=sd3[:, 4 * c:4 * c + 4], in_=ss3[:, 4 * c:4 * c + 4], func=AF.Sqrt, bias=epsb, scale=1.0 / D),
                  reads=[("ss3", c), "epsb"], writes=[("sd3", c)])
            tr.op("dve", lambda e: e.reciprocal(out=rstd3[:, 4 * c:4 * c + 4], in_=sd3[:, 4 * c:4 * c + 4]),
                  reads=[("sd3", c)], writes=[("rstd3", c)])
            for i in range(4):
                t = 4 * c + i
                tr.op("dve", lambda e: e.scalar_tensor_tensor(
                    out=xb[:, i, :], in0=xb[:, i, :], scalar=rstd3[:, t:t + 1], in1=GB, op0=ALU.mult, op1=ALU.mult),
                    reads=[("rstd3", c), "GB"], writes=[("X2", c % 2, i)])
                out_evs.append(tr.dma("sp", s_o[(c % 2) * 4 + i], out_d[t * 128:(t + 1) * 128, :], xb[:, i, :],
                                      reads=[("X2", c % 2, i)]))

        prep_load(0)
        prep(0)
        prep_back(0)
        for c in range(NCH):
            if c + 1 < NCH:
                prep_load(c + 1)
            ffn_in(c, range(0, 2))
            if c + 1 < NCH:
                prep(c + 1)
            ffn_in(c, range(2, 6))
            if c + 1 < NCH:
                prep_back(c + 1)
            ffn_in(c, range(6, 11))
            down_final(c)
        tr._emit_waits("sp", out_evs)
        tr.barrier()
        tr.emit()
    return nc


def _constants():
    bf = ml_dtypes.bfloat16
    ident = np.eye(128, dtype=np.float32).astype(bf)
    k = np.arange(128)[:, None]
    q = np.arange(128)[None, :]
    cmask = np.where(q >= k, 0.0, NEG).astype(np.float32).astype(bf)
    dilb = np.zeros((12, 128, 2, 256), dtype=bf)
    for g in range(3):
        d = DILS[g]
        for h in range(4):
            idx = g * 4 + h
            slope = 2.0 ** (-8.0 * (idx + 1) / 12.0)
            rel_prev = (q - k + 128).astype(np.float64)
            rel_cur = (q - k).astype(np.float64)
            for part, (rel, valid) in enumerate(((rel_cur, k <= q), (rel_prev, k >= q))):
                v = np.where(valid, -slope * rel * d * 8.0, NEG).astype(np.float32)
                hi = v.astype(bf)
                lo = (v - hi.astype(np.float32)).astype(bf)
                dilb[idx, :, 0, part * 128:(part + 1) * 128] = hi
                dilb[idx, :, 1, part * 128:(part + 1) * 128] = lo
    return ident, cmask, dilb


_CACHE = {}


def kernel(x, norm_mix_g, w_in, b_fgt, b_gate, w_dil_out, w_fox_out, w_out,
           norm_ffn_g, w_ffn_in, w_ffn_down, norm_final_g):
    f32 = lambda a: np.ascontiguousarray(np.asarray(a), dtype=np.float32)
    x = f32(x)
    if "nc" not in _CACHE:
        _CACHE["nc"] = build_program()
    nc = _CACHE["nc"]
    ident, cmask, dilb = _constants()
    shared = {
        "norm_mix_g": f32(norm_mix_g).reshape(1, D),
        "w_in": f32(w_in).reshape(D, IN_COLS),
        "b_fgt": f32(b_fgt).reshape(8, 1),
        "b_gate": f32(b_gate).reshape(16, 128),
        "w_dil_out": f32(w_dil_out).reshape(256, D),
        "w_fox_out": f32(w_fox_out).reshape(512, D),
        "w_out": f32(w_out).reshape(D, D),
        "norm_ffn_g": f32(norm_ffn_g).reshape(1, D),
        "w_ffn_in": f32(w_ffn_in).reshape(D, 2 * DFF),
        "w_ffn_down": f32(w_ffn_down).reshape(DFF, D),
        "norm_final_g": f32(norm_final_g).reshape(1, D),
        "c_ident": ident, "c_cmask": cmask, "c_dilb": dilb,
    }
    in_maps = []
    for b in range(8):
        m = dict(shared)
        m["x"] = x[b]
        in_maps.append(m)
    res = run_bass_kernel_spmd(nc, in_maps, core_ids=list(range(8)))
    return np.stack([np.asarray(r["out"], dtype=np.float32).reshape(T, D) for r in res.results], axis=0)
```

```python
import math
from contextlib import ExitStack

import numpy as np
import ml_dtypes
import concourse.bass as bass
import concourse.mybir as mybir
from concourse.bass_utils import run_bass_kernel_spmd

F32 = mybir.dt.float32
BF16 = mybir.dt.bfloat16
AF = mybir.ActivationFunctionType
ALU = mybir.AluOpType

T = 4096
D = 1024
NCH = 8
DFF = 2816
NJ = 22
IN_COLS = 5896
EPS = 1e-6
NEG = -60000.0
DILS = (1, 4, 16)


class Ev:
    __slots__ = ("sem", "val")

    def __init__(self, sem, val):
        self.sem = sem
        self.val = val


class _Rec:
    def __getattr__(self, name):
        def f(*a, **k):
            self.__dict__["call"] = (name, a, k)
            return self
        return f


class TR:
    ENGS = ("pe", "act", "dve", "pool", "sp")

    def __init__(self, nc, stack):
        self.nc = nc
        self.ops = {e: [] for e in self.ENGS}
        self.sem = {}
        self.cnt = {}
        self.seen = {e: {} for e in self.ENGS}
        self.stack = stack
        for e in self.ENGS:
            self.sem[e] = stack.enter_context(nc.semaphore("s_" + e))
            self.cnt[e] = 0
        self.pe_pending = []
        self.lastw = {}
        self.readers = {}
        self.slots = []

    def new_slot(self, name):
        key = "slot_" + name
        self.sem[key] = self.stack.enter_context(self.nc.semaphore("d_" + name))
        self.cnt[key] = 0
        self.slots.append(key)
        return key

    def _deps(self, reads, writes):
        deps = []
        for r in reads:
            w = self.lastw.get(r)
            if w is not None:
                deps.append(w)
        for w_ in writes:
            w = self.lastw.get(w_)
            if w is not None:
                deps.append(w)
            deps.extend(self.readers.get(w_, ()))
        return deps

    def _emit_waits(self, eng, deps):
        best = {}
        for ev in deps:
            if ev.val is None:
                if eng == "pe":
                    continue
                raise RuntimeError("unflushed PE group dependency")
            if ev.sem == "pe" and eng == "pe":
                continue
            if ev.val > best.get(ev.sem, 0):
                best[ev.sem] = ev.val
        for s, v in best.items():
            if v > self.seen[eng].get(s, 0):
                self.seen[eng][s] = v
                semh = self.sem[s]
                self.ops[eng].append(lambda e, semh=semh, v=v: e.wait_ge(semh, v))

    def _record(self, ev, reads, writes):
        for w_ in writes:
            self.lastw[w_] = ev
            self.readers[w_] = []
        for r in reads:
            self.readers.setdefault(r, []).append(ev)

    def op(self, eng, fn0, reads=(), writes=(), inc=True):
        rec = _Rec()
        fn0(rec)
        name_, a_, k_ = rec.call
        fn = lambda e, name_=name_, a_=a_, k_=k_: getattr(e, name_)(*a_, **k_)
        deps = self._deps(reads, writes)
        self._emit_waits(eng, deps)
        if eng == "pe" and not inc:
            ev = Ev("pe", None)
            self.pe_pending.append(ev)
            self.ops[eng].append(lambda e, fn=fn: fn(e))
        else:
            self.cnt[eng] += 1
            v = self.cnt[eng]
            ev = Ev(eng, v)
            semh = self.sem[eng]
            self.ops[eng].append(lambda e, fn=fn, semh=semh: fn(e).then_inc(semh, 1))
            if eng == "pe":
                for p in self.pe_pending:
                    p.val = v
                self.pe_pending = []
        self._record(ev, reads, writes)
        return ev

    def dma(self, eng, slot, out, in_, reads=(), writes=(), **kw):
        deps = self._deps(reads, writes)
        self._emit_waits(eng, deps)
        self.cnt[slot] += 16
        ev = Ev(slot, self.cnt[slot])
        semh = self.sem[slot]
        self.ops[eng].append(lambda e, out=out, in_=in_, semh=semh, kw=kw: e.dma_start(out=out, in_=in_, **kw).then_inc(semh, 16))
        self._record(ev, reads, writes)
        return ev

    def barrier(self):
        assert not self.pe_pending
        evs = [Ev(k, v) for k, v in self.cnt.items() if v > 0]
        for e in self.ENGS:
            self._emit_waits(e, evs)

    def emit(self):
        nc = self.nc
        ops = self.ops
        with nc.Block() as block:
            @block.tensor
            def _(e):
                for f in ops["pe"]:
                    f(e)

            @block.scalar
            def _(e):
                for f in ops["act"]:
                    f(e)

            @block.vector
            def _(e):
                for f in ops["dve"]:
                    f(e)

            @block.gpsimd
            def _(e):
                for f in ops["pool"]:
                    f(e)

            @block.sync
            def _(e):
                for f in ops["sp"]:
                    f(e)


def build_program():
    nc = bass.Bass("TRN2", target_bir_lowering=False)

    def dram(name, shape, dtype=F32, kind="ExternalInput"):
        return nc.dram_tensor(name, shape, dtype, kind=kind).ap()

    x_d = dram("x", [T, D])
    gmix_d = dram("norm_mix_g", [1, D])
    w_in_d = dram("w_in", [D, IN_COLS])
    bfgt_d = dram("b_fgt", [8, 1])
    bgate_d = dram("b_gate", [16, 128])
    wdo_d = dram("w_dil_out", [256, D])
    wfo_d = dram("w_fox_out", [512, D])
    wo_d = dram("w_out", [D, D])
    gffn_d = dram("norm_ffn_g", [1, D])
    wfi_d = dram("w_ffn_in", [D, 2 * DFF])
    wfd_d = dram("w_ffn_down", [DFF, D])
    gfin_d = dram("norm_final_g", [1, D])
    ident_d = dram("c_ident", [128, 128], BF16)
    cmask_d = dram("c_cmask", [128, 128], BF16)
    dilb_d = dram("c_dilb", [12, 128, 2, 256], BF16)
    out_d = dram("out", [T, D], F32, kind="ExternalOutput")

    KB = 1024
    with ExitStack() as st:
        tr = TR(nc, st)
        AR = st.enter_context(nc.sbuf_tensor("arena", [128, 207 * 512], BF16))
        PSALL = st.enter_context(nc.psum_tensor("psall", [128, 4096], F32))
        PS = [PSALL[:, i * 512:(i + 1) * 512] for i in range(8)]

        def view(off, shape, dtype=BF16, p0=0, p1=128):
            n = 1
            for s in shape:
                n *= s
            esz = 4 if dtype == F32 else 2
            assert off % 4 == 0
            a = AR[p0:p1, off // 2: off // 2 + n * esz // 2]
            if dtype == F32:
                a = a.bitcast(F32)
            if len(shape) == 2:
                a = a.rearrange("p (a b) -> p a b", a=shape[0])
            elif len(shape) == 3:
                a = a.rearrange("p (a b c) -> p a b c", a=shape[0], b=shape[1])
            return a

        H4 = view(0, [8, 8, 512])
        OA = view(64 * KB, [2, T])
        C0 = 80 * KB
        ident = view(C0, [128])
        cmask = view(C0 + 256, [128])
        ones32 = view(C0 + 512, [64], F32)
        epsb = view(C0 + 768, [1], F32)
        ss = view(C0 + 1024, [32], F32)
        sd = view(C0 + 1152, [32], F32)
        rstd = view(C0 + 1280, [32], F32)
        bgate = view(C0 + 1408, [16], F32)
        ss2 = view(C0 + 1536, [32], F32)
        sd2 = view(C0 + 1664, [32], F32)
        rstd2 = view(C0 + 1792, [32], F32)
        ss3 = view(C0 + 1920, [32], F32)
        sd3 = view(C0 + 2048, [32], F32)
        rstd3 = view(C0 + 2176, [32], F32)
        nbf = view(C0 + 2304, [1], F32)
        onesb = view(C0 + 2560, [512], BF16)
        GA = view(84 * KB, [D], F32)
        GB = view(88 * KB, [D], F32)
        OB = view(92 * KB, [4, T])
        L0 = 124 * KB

        psb = [p.bitcast(BF16) for p in PS]

        tr.dma("sp", tr.new_slot("c1"), ident, ident_d, writes=["ident"])
        tr.dma("sp", tr.new_slot("c2"), cmask, cmask_d, writes=["cmask"])
        tr.op("dve", lambda e: e.memset(ones32, 1.0), writes=["ones32"])
        tr.op("dve", lambda e: e.memset(epsb, EPS), writes=["epsb"])
        tr.op("dve", lambda e: e.memset(onesb, 1.0), writes=["onesb"])
        for s_ in (ss, ss2, ss3):
            tr.op("dve", lambda e, s_=s_: e.memset(s_, 0.0), writes=[("ssinit", id(s_))])
        s_g = tr.new_slot("gain")
        tr.dma("sp", s_g, GA, gmix_d.partition_broadcast(128), writes=["GA"])
        tr.barrier()

        xt = [view(L0 + i * 4 * KB, [D], F32) for i in range(8)]
        hn = [view(L0 + 32 * KB + i * 2 * KB, [D], BF16) for i in range(2)]
        junk = view(L0 + 36 * KB, [D], BF16)
        s_x = [tr.new_slot("x%d" % i) for i in range(8)]

        def rms_stats(c, srcs, ssv, sdv, rstdv, tagp):
            for i in range(4):
                t = 4 * c + i
                tr.op("act", lambda e, i=i, t=t: e.activation(out=junk, in_=srcs[(c % 2) * 4 + i], func=AF.Square,
                                                               accum_out=ssv[:, t:t + 1]),
                      reads=[(tagp, "src", (c % 2) * 4 + i)], writes=["junk", (tagp, "ss", c)])
            tr.op("act", lambda e: e.activation(out=sdv[:, 4 * c:4 * c + 4], in_=ssv[:, 4 * c:4 * c + 4], func=AF.Sqrt,
                                                bias=epsb, scale=1.0 / D),
                  reads=[(tagp, "ss", c), "epsb"], writes=[(tagp, "sd", c)])
            tr.op("dve", lambda e: e.reciprocal(out=rstdv[:, 4 * c:4 * c + 4], in_=sdv[:, 4 * c:4 * c + 4]),
                  reads=[(tagp, "sd", c)], writes=[(tagp, "rstd", c)])

        def p0_load_stats(c):
            for i in range(4):
                t = 4 * c + i
                xi = (c % 2) * 4 + i
                tr.dma("sp" if i % 2 == 0 else "pool", s_x[xi], xt[xi], x_d[t * 128:(t + 1) * 128, :], writes=[("p0", "src", xi)])
            rms_stats(c, xt, ss, sd, rstd, "p0")

        s_w = [tr.new_slot("w%d" % i) for i in range(4)]
        QT0 = view(156 * KB + 17024, [T])
        KT0 = view(156 * KB + 17024 + 8 * KB, [T])
        WV0 = view(156 * KB + 17024 + 16 * KB + 8448, [8, 256])
        tr.dma("pool", s_w[2], WV0, w_in_d[:, 1536:1792].rearrange("(k p) c -> p k c", p=128), writes=["WV"])

        def p0_N(c, i):
            t = 4 * c + i
            b = t % 2
            xi = (c % 2) * 4 + i
            tr.op("dve", lambda e: e.scalar_tensor_tensor(
                out=hn[b], in0=xt[xi], scalar=rstd[:, t:t + 1], in1=GA, op0=ALU.mult, op1=ALU.mult),
                reads=[("p0", "src", xi), ("p0", "rstd", c), "GA"], writes=[("hn", b)])
            pb = t % 4
            for kc in range(8):
                tr.op("pe", lambda e: e.transpose(
                    out=psb[pb][:, kc * 128:(kc + 1) * 128], in_=hn[b][:, kc * 128:(kc + 1) * 128], identity=ident),
                    reads=[("hn", b), "ident"], writes=[("ps", pb)], inc=(kc == 7))

        def p0_E(c, i):
            pb = (4 * c + i) % 4
            tr.op("dve", lambda e: e.tensor_copy(
                      out=H4[:, c, :, i * 128:(i + 1) * 128],
                      in_=psb[pb].rearrange("p (k t) -> p k t", k=8)),
                  reads=[("ps", pb)], writes=[("H", c)])
            if i == 3:
                for ft in range(2):
                    vkey = "QT" if ft == 0 else "KT"
                    xkey = "QTx" if ft == 0 else "KTx"
                    vt = QT0 if ft == 0 else KT0
                    pbv = 4 + ft
                    for kc in range(8):
                        tr.op("pe", lambda e: e.matmul(PS[pbv][:, :], lhsT=WV0[:, kc, ft * 128:(ft + 1) * 128], rhs=H4[:, c, kc, :],
                                                       start=(kc == 0), stop=(kc == 7)),
                              reads=["WV", ("H", c)], writes=[("ps", pbv)], inc=(kc == 7))
                    tr.op("act", lambda e: e.activation(out=vt[:, c * 512:(c + 1) * 512], in_=PS[pbv][:, :], func=AF.Copy),
                          reads=[("ps", pbv)], writes=[vkey, xkey])

        seq = [(c, i) for c in range(NCH) for i in range(4)]
        p0_load_stats(0)
        p0_load_stats(1)
        p0_N(0, 0)
        for k in range(len(seq)):
            if k + 1 < len(seq):
                c1, i1 = seq[k + 1]
                if i1 == 0 and c1 + 1 < NCH:
                    p0_load_stats(c1 + 1)
                p0_N(c1, i1)
            p0_E(*seq[k])

        tr.barrier()
        tr.dma("sp", tr.new_slot("bg"), bgate, bgate_d.rearrange("c p -> p c"), writes=["bgate"], allow_slow_non_contiguous=True)

        ACC = view(92 * KB, [4, T], F32, 0, 128)
        VA = view(156 * KB, [32, 4, 65])
        QT = view(156 * KB + 16640 + 384, [T])
        KT = view(156 * KB + 16640 + 384 + 8 * KB, [T])
        QTF, KTF = QT, KT
        WB = 156 * KB + 16640 + 384 + 16 * KB
        WQ = view(WB, [8, 264])
        WK = view(WB + 4224, [8, 264])
        WV = view(WB + 8448, [8, 256])
        BTS = [view(WB + 12544 + i * 1024, [2, 256]) for i in range(2)]
        PT = [view(WB + 14592 + i * 512, [256]) for i in range(4)]
        assert WB + 14592 + 4 * 512 <= 207 * KB
        KTB = view(64 * KB, [T])
        s_bts = [tr.new_slot("bts%d" % i) for i in range(2)]
        RD = view(WB, [512], F32)
        BCS = view(WB + 2 * KB, [512], F32)
        TMPO = [view(WB + 4 * KB + i * KB, [512], BF16) for i in range(2)]

        s_bt = tr.new_slot("bt")
        s_mv = [tr.new_slot("mv%d" % i) for i in range(2)]

        tr.op("dve", lambda e: e.memset(VA[:, :, :, 64:65], 1.0), writes=["VAones"])
        tr.op("dve", lambda e: e.memset(KTB[0:64, :], 0.0), writes=["KTB"])
        tr.op("dve", lambda e: e.memset(WQ[:, :, 256:264], 0.0), writes=["WQpad"])
        tr.op("dve", lambda e: e.memset(WK[:, :, 256:264], 0.0), writes=["WKpad"])

        def hcols(kc, start, stride, count):
            last = start + (count - 1) * stride
            c0, c1 = start // 512, last // 512
            if c0 == c1:
                o = start % 512
                return H4[:, c0, kc, o:o + (count - 1) * stride + 1:stride]
            assert stride == 16 and count == 128 and start % 2048 < 16
            return H4[:, c0:c0 + 4, kc, (start % 512)::16]

        def wcols(col0, n):
            return w_in_d[:, col0:col0 + n].rearrange("(k p) c -> p k c", p=128)

        def qk_proj(wt, wtag, wc0, dst, dtag, pbanks):
            for c in range(NCH):
                pb = pbanks[c % len(pbanks)]
                for kc in range(8):
                    tr.op("pe", lambda e, c=c, kc=kc, pb=pb: e.matmul(
                        PS[pb][0:70, :], lhsT=wt[:, kc, wc0:wc0 + 70], rhs=H4[:, c, kc, :],
                        start=(kc == 0), stop=(kc == 7)),
                        reads=[wtag, ("H", c)], writes=[("ps", pb)], inc=(kc == 7))
                if c % 2 == 0:
                    tr.op("dve", lambda e, c=c, pb=pb: e.tensor_copy(out=dst[0:64, c * 512:(c + 1) * 512], in_=PS[pb][0:64, :]),
                          reads=[("ps", pb)], writes=[dtag])
                else:
                    tr.op("act", lambda e, c=c, pb=pb: e.activation(out=dst[0:64, c * 512:(c + 1) * 512], in_=PS[pb][0:64, :], func=AF.Copy),
                          reads=[("ps", pb)], writes=[dtag])

        def normalize_out(src_num, src_den_row, rtag, OX, h, c, pbank):
            tr.op("dve", lambda e: e.reciprocal(out=RD[64:65, :], in_=src_den_row), reads=[rtag], writes=["RD"])
            tr.op("pe", lambda e: e.matmul(PS[pbank][0:64, :], lhsT=ones32[64:65, 0:64], rhs=RD[64:65, :], start=True, stop=True),
                  reads=["RD", "ones32"], writes=[("ps", pbank)])
            tr.op("act", lambda e: e.activation(out=BCS[0:64, :], in_=PS[pbank][0:64, :], func=AF.Copy),
                  reads=[("ps", pbank)], writes=["BCS"])
            if h % 2 == 0:
                tr.op("dve", lambda e: e.tensor_tensor(out=OX[0:64, h // 2, c * 512:(c + 1) * 512], in0=src_num, in1=BCS[0:64, :], op=ALU.mult),
                      reads=[rtag, "BCS"], writes=[("OX", id(OX), h, c)])
            else:
                b = c % 2
                tr.op("dve", lambda e: e.tensor_tensor(out=TMPO[b][0:64, :], in0=src_num, in1=BCS[0:64, :], op=ALU.mult),
                      reads=[rtag, "BCS"], writes=[("TMPO", b)])
                tr.dma("sp", s_mv[b], OX[64:128, h // 2, c * 512:(c + 1) * 512], TMPO[b][0:64, :],
                       reads=[("TMPO", b)], writes=[("OX", id(OX), h, c)])

        for g in range(3):
            d = DILS[g]
            tr.dma("pool", s_w[0], WQ[:, :, 0:256], wcols(g * 256, 256), writes=["WQ"])
            tr.dma("pool", s_w[1], WK[:, :, 0:256], wcols(768 + g * 256, 256), writes=["WK"])
            if g > 0:
                tr.dma("pool", s_w[2], WV, wcols(1536 + g * 256, 256), writes=["WV"])
            VT = (QTF, KTF)
            for ft in (range(2) if g > 0 else ()):
                vkey = "QT" if ft == 0 else "KT"
                xkey = "QTx" if ft == 0 else "KTx"
                for c in range(NCH):
                    pb = 4 + (c % 2)
                    for kc in range(8):
                        tr.op("pe", lambda e: e.matmul(PS[pb][:, :], lhsT=WV[:, kc, ft * 128:(ft + 1) * 128], rhs=H4[:, c, kc, :],
                                                       start=(kc == 0), stop=(kc == 7)),
                              reads=["WV", ("H", c)], writes=[("ps", pb)], inc=(kc == 7))
                    if c % 2 == 0:
                        tr.op("dve", lambda e: e.tensor_copy(out=VT[ft][:, c * 512:(c + 1) * 512], in_=PS[pb][:, :]),
                              reads=[("ps", pb)], writes=[vkey, xkey])
                    else:
                        tr.op("act", lambda e: e.activation(out=VT[ft][:, c * 512:(c + 1) * 512], in_=PS[pb][:, :], func=AF.Copy),
                              reads=[("ps", pb)], writes=[vkey, xkey])
            for j in range(32):
                n, r = j // d, j % d
                pb = 4 + (j % 2)
                qs = n * 128 * d + r
                for ft in range(2):
                    vkey = "QT" if ft == 0 else "KT"
                    tr.op("pe", lambda e: e.transpose(out=psb[pb][:, ft * 128:(ft + 1) * 128], in_=VT[ft][:, qs:qs + 127 * d + 1:d], identity=ident),
                          reads=[vkey, "ident"], writes=[("ps", pb)], inc=(ft == 1))
                if j % 2:
                    tr.op("dve", lambda e: e.tensor_copy(out=VA[:, j, :, 0:64], in_=psb[pb][:, 0:256].rearrange("p (h d) -> p h d", h=4)),
                          reads=[("ps", pb)], writes=["VA"])
                else:
                    tr.op("act", lambda e: e.activation(out=VA[:, j, :, 0:64], in_=psb[pb][:, 0:256].rearrange("p (h d) -> p h d", h=4), func=AF.Copy),
                          reads=[("ps", pb)], writes=["VA"])
            tr.op("dve", lambda e: e.memset(KT[64:128, :], 0.0), reads=[], writes=["KTx", "KT"])
            for h in range(4):
                BT = BTS[h % 2]
                tr.dma("sp", s_bts[h % 2], BT, dilb_d[g * 4 + h], writes=[("BT", h % 2)])
                if h % 2 == 0:
                    p = h // 2
                    for c in range(NCH):
                        pbq, pbk = 4, 5
                        for kc in range(8):
                            tr.op("pe", lambda e: e.matmul(PS[pbq][:, :], lhsT=WQ[:, kc, p * 128:(p + 1) * 128], rhs=H4[:, c, kc, :],
                                                           start=(kc == 0), stop=(kc == 7)),
                                  reads=["WQ", ("H", c)], writes=[("ps", pbq)], inc=(kc == 7))
                        tr.op("dve", lambda e: e.tensor_copy(out=QT[:, c * 512:(c + 1) * 512], in_=PS[pbq][:, :]),
                              reads=[("ps", pbq)], writes=["QT"])
                        for kc in range(8):
                            tr.op("pe", lambda e: e.matmul(PS[pbk][:, :], lhsT=WK[:, kc, p * 128:(p + 1) * 128], rhs=H4[:, c, kc, :],
                                                           start=(kc == 0), stop=(kc == 7)),
                                  reads=["WK", ("H", c)], writes=[("ps", pbk)], inc=(kc == 7))
                        tr.op("act", lambda e: e.activation(out=KT[0:64, c * 512:(c + 1) * 512], in_=PS[pbk][0:64, :], func=AF.Copy),
                              reads=[("ps", pbk)], writes=["KT"])
                        tr.op("dve", lambda e: e.tensor_copy(out=KTB[64:128, c * 512:(c + 1) * 512], in_=PS[pbk][64:128, :]),
                              reads=[("ps", pbk)], writes=["KTB"])
                KTS = KT if h % 2 == 0 else KTB
                ktkey = "KT" if h % 2 == 0 else "KTB"
                nb = 32 // d
                units = [(r, n) for r in range(d) for n in range(nb)]

                def emit_S(u):
                    r, n = units[u]
                    W = 256 if n + 1 < nb else 128
                    sb_ = u % 3
                    qs = n * 128 * d + r
                    rhs = QT[0:128, qs:qs + (W - 1) * d + 1:d]
                    lhsT = KTS[0:128, qs:qs + 127 * d + 1:d]
                    o_ps = PS[sb_][:, 0:W]
                    tr.op("pe", lambda e: e.matmul(o_ps, lhsT=lhsT, rhs=rhs, start=True, stop=False),
                          reads=["QT", ktkey, "KTx"], writes=[("ps", sb_)], inc=False)
                    tr.op("pe", lambda e: e.matmul(o_ps, lhsT=ident, rhs=BT[:, 0, 0:W], start=False, stop=False),
                          reads=[("BT", h % 2), "ident"], writes=[("ps", sb_)], inc=False)
                    tr.op("pe", lambda e: e.matmul(o_ps, lhsT=ident, rhs=BT[:, 1, 0:W], start=False, stop=True),
                          reads=[("BT", h % 2), "ident"], writes=[("ps", sb_)], inc=True)

                def emit_E(u):
                    r, n = units[u]
                    W = 256 if n + 1 < nb else 128
                    sb_ = u % 3
                    tr.op("act", lambda e: e.activation(out=PT[u % 4][:, 0:W], in_=PS[sb_][:, 0:W], func=AF.Exp, scale=0.125),
                          reads=[("ps", sb_)], writes=[("PT", u % 4)])

                def emit_PV(u):
                    r, n = units[u]
                    ob = 6 + ((u // 4) % 2)
                    jj = u % 4
                    o_ps = PS[ob][0:65, jj * 128:(jj + 1) * 128]
                    if n >= 1:
                        tr.op("pe", lambda e: e.matmul(o_ps, lhsT=VA[:, (n - 1) * d + r, h, :], rhs=PT[(u - 1) % 4][:, 128:256], start=True, stop=False),
                              reads=["VA", "VAones", ("PT", (u - 1) % 4)], writes=[("ps", ob)], inc=False)
                    tr.op("pe", lambda e: e.matmul(o_ps, lhsT=VA[:, n * d + r, h, :], rhs=PT[u % 4][:, 0:128], start=(n == 0), stop=True),
                          reads=["VA", "VAones", ("PT", u % 4)], writes=[("ps", ob)], inc=True)
                    if jj == 3:
                        accv = ACC[0:65, h, :]
                        if g == 0:
                            n0 = n - 3
                            dst = accv[:, n0 * 128:n0 * 128 + 512]
                            tr.op("dve", lambda e: e.tensor_copy(out=dst, in_=PS[ob][0:65, :]),
                                  reads=[("ps", ob)], writes=[("ACC", h)])
                        else:
                            if g == 1:
                                n0 = n - 3
                                dst = accv.rearrange("p (n i r) -> p n i r", n=8, r=4)[:, n0:n0 + 4, :, r]
                                src = PS[ob][0:65, :].rearrange("p (a i) -> p a i", a=4)
                            else:
                                r0 = r - 1
                                dst = accv.rearrange("p (n i r) -> p r n i", n=2, r=16)[:, r0:r0 + 2, :, :]
                                src = PS[ob][0:65, :].rearrange("p (a b i) -> p a b i", a=2, b=2)
                            tr.op("dve", lambda e: e.tensor_tensor(out=dst, in0=src, in1=dst, op=ALU.add),
                                  reads=[("ps", ob)], writes=[("ACC", h)])

                LA = 2
                for u in range(LA):
                    emit_S(u)
                for u in range(32):
                    if u + LA < 32:
                        emit_S(u + LA)
                    emit_E(u)
                    emit_PV(u)
        tr.barrier()
        DN0 = 200064
        LNS = [view(DN0 + i * 2 * KB, [512], F32) for i in range(2)]
        BC2 = [view(DN0 + 4 * KB + i * 2 * KB, [512], F32) for i in range(2)]
        TM2 = [view(DN0 + 8 * KB + i * KB, [512], BF16) for i in range(2)]
        assert DN0 + 10 * KB <= 207 * KB

        def dn_step(stp):
            h, c = stp // NCH, stp % NCH
            b = stp % 2
            cs = slice(c * 512, (c + 1) * 512)
            pbank = 6 + b
            tr.op("pe", lambda e: e.matmul(PS[pbank][0:64, :], lhsT=ones32[64:65, 0:64], rhs=ACC[64:65, h, cs], start=True, stop=True),
                  reads=["ones32"], writes=[("ps", pbank)])
            tr.op("act", lambda e: e.activation(out=LNS[b][0:64, :], in_=PS[pbank][0:64, :], func=AF.Ln),
                  reads=[("ps", pbank)], writes=[("LNS", b)])
            tr.op("act", lambda e: e.activation(out=BC2[b][0:64, :], in_=LNS[b][0:64, :], func=AF.Exp, scale=-1.0),
                  reads=[("LNS", b)], writes=[("BC2", b)])
            if h % 2 == 0:
                tr.op("dve", lambda e: e.tensor_tensor(out=OA[0:64, h // 2, cs], in0=ACC[0:64, h, cs], in1=BC2[b][0:64, :], op=ALU.mult),
                      reads=[("BC2", b)], writes=[("OA", h, c)])
            else:
                tr.op("dve", lambda e: e.tensor_tensor(out=TM2[b][0:64, :], in0=ACC[0:64, h, cs], in1=BC2[b][0:64, :], op=ALU.mult),
                      reads=[("BC2", b)], writes=[("TM2", b)])
                tr.dma("sp", s_mv[b], OA[64:128, h // 2, cs], TM2[b][0:64, :], reads=[("TM2", b)], writes=[("OA", h, c)])

        VB = view(L0, [32, 4, 65])
        QTS = [view(L0 + 16640 + 384, [T]), None]
        KTS = [view(L0 + 16640 + 384 + 8 * KB, [T]), None]
        WB = L0 + 16640 + 384 + 16 * KB
        WQP = view(WB, [8, 128])
        WKP = view(WB + 2048, [8, 128])
        WV = view(WB + 4608, [8, 256])
        CQ = view(WB + 8704, [T])
        ZB = WB + 8704 + 8 * KB
        QTS[1] = view(ZB, [T])
        KTS[1] = view(ZB + 8 * KB, [T])
        CT = [view(ZB + i * 2 * KB, [512], F32) for i in range(3)]
        HML = [view(ZB + 6 * KB + i * KB, [512], BF16) for i in range(3)]
        SB_ = ZB + 16 * KB
        WF = view(SB_, [8, 8])
        NBF = view(SB_ + 128, [1], F32)
        BFG = view(SB_ + 192, [1], F32)
        CARRY = view(SB_ + 200, [1], F32)
        PT2 = [view(SB_ + 256 + i * 2 * KB, [1024]) for i in range(3)]
        N0 = SB_ + 256 + 6 * KB
        RD = view(N0, [512], F32)
        BCS = view(N0 + 2 * KB, [512], F32)
        TMPO = [view(N0 + 4 * KB + i * KB, [512], BF16) for i in range(2)]
        assert N0 + 6 * KB <= 207 * KB, (N0 + 6 * KB) / KB

        OBK = (4, 5, 6, 7)
        BCS2 = [RD, BCS]
        s_bc = [tr.new_slot("bc%d" % i) for i in range(2)]
        s_cq = [tr.new_slot("cq%d" % i) for i in range(12)]
        s_hml = [tr.new_slot("hml%d" % i) for i in range(3)]
        s_f = tr.new_slot("wf")
        s_wqh = [tr.new_slot("wqh%d" % i) for i in range(2)]
        s_wkh = [tr.new_slot("wkh%d" % i) for i in range(2)]

        E0, X8, CC = CT[0], CT[1], CT[2]
        tr.dma("pool", s_f, WF, wcols(3840, 8), writes=["WF"])
        tr.dma("sp", tr.new_slot("bf"), BFG[0:8, :], bfgt_d, writes=["BFG"])
        tr.op("dve", lambda e: e.tensor_scalar(out=NBF[0:8, :], in0=BFG[0:8, :], scalar1=-1.0, scalar2=None, op0=ALU.mult),
              reads=["BFG"], writes=["NBF"])
        for c in range(NCH):
            for q_ in range(4):
                dn_step(4 * c + q_)
            pb = 4 + (c % 2)
            cs = slice(c * 512, (c + 1) * 512)
            for kc in range(8):
                tr.op("pe", lambda e: e.matmul(PS[pb][0:8, :], lhsT=WF[:, kc, 0:8], rhs=H4[:, c, kc, :],
                                               start=(kc == 0), stop=(kc == 7)),
                      reads=["WF", ("H", c)], writes=[("ps", pb)], inc=(kc == 7))
            tr.op("act", lambda e: e.activation(out=E0[0:8, :], in_=PS[pb][0:8, :], func=AF.Exp, bias=NBF[0:8, :], scale=-1.0),
                  reads=[("ps", pb), "NBF"], writes=["E0"])
            tr.op("act", lambda e: e.activation(out=E0[0:8, :], in_=E0[0:8, :], func=AF.Ln, bias=ones32[0:8, 0:1], scale=1.0),
                  reads=["E0", "ones32"], writes=["E0"])
            tr.op("dve", lambda e: e.tensor_scalar(out=X8[0:8, :], in0=E0[0:8, :], scalar1=-8.0, scalar2=None, op0=ALU.mult),
                  reads=["E0"], writes=["X8"])
            if c == 0:
                tr.op("dve", lambda e: e.tensor_tensor_scan(out=CC[0:8, :], data0=onesb[0:8, :], data1=X8[0:8, :], initial=0.0,
                                                            op0=ALU.mult, op1=ALU.add),
                      reads=["X8", "onesb"], writes=["CC"])
            else:
                tr.op("dve", lambda e: e.tensor_tensor_scan(out=CC[0:8, :], data0=onesb[0:8, :], data1=X8[0:8, :], initial=CARRY[0:8, :],
                                                            op0=ALU.mult, op1=ALU.add),
                      reads=["X8", "onesb", "CARRY"], writes=["CC"])
            tr.op("dve", lambda e: e.tensor_copy(out=CARRY[0:8, :], in_=CC[0:8, 511:512]), reads=["CC"], writes=["CARRY"])
            tr.op("dve", lambda e: e.tensor_copy(out=HML[0][0:8, :], in_=CC[0:8, :]), reads=["CC"], writes=[("HML", 0)])
            tr.op("dve", lambda e: e.tensor_tensor(out=X8[0:8, :], in0=CC[0:8, :], in1=HML[0][0:8, :], op=ALU.subtract),
                  reads=["CC", ("HML", 0)], writes=["X8"])
            tr.op("dve", lambda e: e.tensor_copy(out=HML[1][0:8, :], in_=X8[0:8, :]), reads=["X8"], writes=[("HML", 1)])
            tr.op("dve", lambda e: e.tensor_tensor(out=E0[0:8, :], in0=X8[0:8, :], in1=HML[1][0:8, :], op=ALU.subtract),
                  reads=["X8", ("HML", 1)], writes=["E0"])
            tr.op("dve", lambda e: e.tensor_copy(out=HML[2][0:8, :], in_=E0[0:8, :]), reads=["E0"], writes=[("HML", 2)])
            for p_ in range(3):
                tr.dma("sp", s_hml[p_], CQ[p_ * 8:p_ * 8 + 8, cs], HML[p_][0:8, :], reads=[("HML", p_)], writes=["CQ"])
        tr.barrier()
        tr.op("dve", lambda e: e.memset(VB[:, :, :, 64:65], 1.0), writes=["VBones"])
        for i in range(2):
            eng_ = "dve"
            tr.op(eng_, lambda e: e.memset(QTS[i][64:128, :], -1.0), writes=[("QTx", i)])
            tr.op(eng_, lambda e: e.memset(KTS[i][64:128, :], 0.0), writes=[("KTx", i)])
            tr.op(eng_, lambda e: e.memset(KTS[i][64:70, :], 1.0), writes=[("KTx", i)])

        STG = [view(N0 + 6 * KB + i * KB, [512]) for i in range(4)]
        assert N0 + 10 * KB <= 207 * KB
        s_stg = [tr.new_slot("stg%d" % i) for i in range(4)]

        def pair_setup(p):
            tr.dma("pool", s_wqh[0], WQP, wcols(2304 + p * 128, 128), writes=["WQP"])
            tr.dma("pool", s_wkh[0], WKP, wcols(2816 + p * 128, 128), writes=["WKP"])
            for wb_ in range(2):
                h = 2 * p + wb_
                for p_ in range(3):
                    tr.dma("pool", s_cq[wb_ * 6 + p_], QTS[wb_][64 + p_:65 + p_, :], CQ[p_ * 8 + h:p_ * 8 + h + 1, :], reads=["CQ"], writes=[("QTx", wb_)])
                    tr.dma("pool", s_cq[wb_ * 6 + 3 + p_], KTS[wb_][67 + p_:68 + p_, :], CQ[p_ * 8 + h:p_ * 8 + h + 1, :], reads=["CQ"], writes=[("KTx", wb_)])

        def pair_proj(p):
            k = 0
            for c in range(NCH):
                cs = slice(c * 512, (c + 1) * 512)
                for qi, (wt, wkey, dsts, dk) in enumerate(((WQP, "WQP", QTS, "QT"), (WKP, "WKP", KTS, "KT"))):
                    pb = 4 + (k % 2)
                    sg = 2 * (c % 2) + qi
                    k += 1
                    for kc in range(8):
                        tr.op("pe", lambda e: e.matmul(PS[pb][:, :], lhsT=wt[:, kc, :], rhs=H4[:, c, kc, :],
                                                       start=(kc == 0), stop=(kc == 7)),
                              reads=[wkey, ("H", c)], writes=[("ps", pb)], inc=(kc == 7))
                    tr.op("act", lambda e: e.activation(out=dsts[0][0:64, cs], in_=PS[pb][0:64, :], func=AF.Copy),
                          reads=[("ps", pb)], writes=[(dk, 0)])
                    tr.op("act", lambda e: e.activation(out=STG[sg][64:128, :], in_=PS[pb][64:128, :], func=AF.Copy),
                          reads=[("ps", pb)], writes=[("STG", sg)])
                    tr.dma("sp", s_stg[sg], dsts[1][0:64, cs], STG[sg][64:128, :], reads=[("STG", sg)], writes=[(dk, 1)])

        WGP = [view(84 * KB, [8, 256]), view(88 * KB, [8, 256])]
        s_gb = tr.new_slot("gb")
        deferred = []
        for h in range(8):
            half, hh = h // 4, h % 4
            qb = h % 2
            QT, KT = QTS[qb], KTS[qb]
            if hh == 0:
                tr.dma("pool", s_w[2], WV, wcols(3328 + half * 256, 256), writes=["WV"])
                for t in range(32):
                    pb = 4 + (t % 2)
                    for kc in range(8):
                        tr.op("pe", lambda e: e.matmul(
                            PS[pb][:, 0:256], lhsT=H4[:, t // 4, kc, (t % 4) * 128:(t % 4 + 1) * 128], rhs=WV[:, kc, :],
                            start=(kc == 0), stop=(kc == 7)),
                            reads=["WV", ("H", t // 4)], writes=[("ps", pb)], inc=(kc == 7))
                    tr.op("act", lambda e: e.activation(out=VB[:, t, :, 0:64], in_=PS[pb][:, 0:256].rearrange("p (h d) -> p h d", h=4), func=AF.Copy),
                          reads=[("ps", pb)], writes=["VB"])
            if qb == 0:
                pair_setup(h // 2)
                pair_proj(h // 2)
                while deferred:
                    deferred.pop(0)()
                if h == 6:
                    tr.dma("pool", s_g, WGP[0], wcols(3848, 256), writes=["GA"])
                    tr.dma("pool", s_gb, WGP[1], wcols(3848 + 1024, 256), writes=["GB"])
            pending = []
            groups = []
            for qc in range(NCH):
                for b in range(0, 4 * qc, 2):
                    groups.append([(qc, b), (qc, b + 1)])
                for b in range(4 * qc, 4 * qc + 4):
                    groups.append([(qc, b)])

            def emit_S(gi):
                bp = 2 * (gi % 2)
                for ti, (qc, b) in enumerate(groups[gi]):
                    r = b - 4 * qc
                    col0 = max(0, r) * 128
                    sb_ = bp + ti
                    tr.op("pe", lambda e: e.matmul(PS[sb_][:, col0:512], lhsT=KT[0:128, b * 128:(b + 1) * 128],
                                                   rhs=QT[0:128, qc * 512 + col0:(qc + 1) * 512], start=True, stop=(r < 0)),
                          reads=[("QT", qb), ("KT", qb), ("QTx", qb), ("KTx", qb)], writes=[("ps", sb_)], inc=(r < 0))
                    if r >= 0:
                        tr.op("pe", lambda e: e.matmul(PS[sb_][:, col0:col0 + 128], lhsT=ident, rhs=cmask, start=False, stop=True),
                              reads=["ident", "cmask"], writes=[("ps", sb_)], inc=True)

            def emit_E(gi):
                bp = 2 * (gi % 2)
                pt = PT2[gi % 3]
                grp = groups[gi]
                if len(grp) == 2:
                    tr.op("act", lambda e: e.activation(out=pt[:, 0:1024], in_=PSALL[:, bp * 512:(bp + 2) * 512], func=AF.Exp, scale=0.125),
                          reads=[("ps", bp), ("ps", bp + 1)], writes=[("PT2", gi % 3)])
                else:
                    qc, b = grp[0]
                    col0 = max(0, b - 4 * qc) * 128
                    tr.op("act", lambda e: e.activation(out=pt[:, col0:512], in_=PS[bp][:, col0:512], func=AF.Exp, scale=0.125),
                          reads=[("ps", bp)], writes=[("PT2", gi % 3)])

            def emit_PV(gi):
                pt = PT2[gi % 3]
                for ti, (qc, b) in enumerate(groups[gi]):
                    col0 = max(0, b - 4 * qc) * 128
                    ob = OBK[(h * 8 + qc) % 4]
                    last = (b == 4 * qc + 3)
                    tr.op("pe", lambda e: e.matmul(PS[ob][0:65, col0:512], lhsT=VB[:, b, hh, :], rhs=pt[:, ti * 512 + col0:(ti + 1) * 512],
                                                   start=(b == 0), stop=last),
                          reads=["VB", "VBones", ("PT2", gi % 3)], writes=[("ps", ob)], inc=(last or ti == len(groups[gi]) - 1))
                    if last:
                        gc = h * 8 + qc
                        b2 = gc % 2
                        cs = slice(qc * 512, (qc + 1) * 512)
                        tr.op("dve", lambda e: e.reciprocal(out=BCS2[b2][64:65, :], in_=PS[ob][64:65, :]),
                              reads=[("ps", ob)], writes=[("BCSr", b2)])

                        def fin_tail(ob=ob, b2=b2, cs=cs, qc=qc, h=h):
                            tr.dma("sp", s_bc[b2], BCS2[b2][0:64, :], BCS2[b2][64:65, :].unsqueeze(1).broadcast_to([1, 64, 512]),
                                   reads=[("BCSr", b2)], writes=[("BCS", b2)])
                            if h % 2 == 0:
                                tr.op("dve", lambda e: e.tensor_tensor(out=OB[0:64, h // 2, cs], in0=PS[ob][0:64, :], in1=BCS2[b2][0:64, :], op=ALU.mult),
                                      reads=[("ps", ob), ("BCS", b2)], writes=[("OB", h, qc)])
                            else:
                                tr.op("dve", lambda e: e.tensor_tensor(out=TMPO[b2][0:64, :], in0=PS[ob][0:64, :], in1=BCS2[b2][0:64, :], op=ALU.mult),
                                      reads=[("ps", ob), ("BCS", b2)], writes=[("TMPO", b2)])
                                tr.dma("sp", s_mv[b2], OB[64:128, h // 2, cs], TMPO[b2][0:64, :],
                                       reads=[("TMPO", b2)], writes=[("OB", h, qc)])
                        if h % 2 == 1 and qc >= NCH - 2 and h < 7:
                            deferred.append(fin_tail)
                        else:
                            fin_tail()

            emit_S(0)
            for gi in range(len(groups)):
                if gi + 1 < len(groups):
                    emit_S(gi + 1)
                emit_E(gi)
                if gi >= 1:
                    emit_PV(gi - 1)
            emit_PV(len(groups) - 1)
        tr.barrier()

        WG = view(L0, [8, 2048])
        WDO = view(L0 + 32 * KB, [2, D])
        WFO = view(L0 + 36 * KB, [4, D])
        WO = view(L0 + 44 * KB, [8, D])
        MT = view(L0 + 60 * KB, [8, 512])
        SG = [view(L0 + 68 * KB + i * 2 * KB, [512], F32) for i in range(2)]
        M12 = [view(L0 + 72 * KB + i * 2 * KB, [512], F32) for i in range(2)]
        sl = [tr.new_slot("p2w%d" % i) for i in range(6)]
        s_wg = [tr.new_slot("wg%d" % i) for i in range(8)]
        s_wdo = [tr.new_slot("wdo%d" % i) for i in range(4)]
        s_wfo = [tr.new_slot("wfo%d" % i) for i in range(4)]
        for k in range(4):
            for gi in range(2):
                if k == 0:
                    continue
                tr.dma("pool", s_wg[gi * 4 + k], WG[:, :, gi * 1024 + k * 256:gi * 1024 + (k + 1) * 256],
                       wcols(3848 + gi * 1024 + k * 256, 256), writes=[("WG", gi, k)])
            tr.dma("pool", s_wdo[k], WDO[:, :, k * 256:(k + 1) * 256], wdo_d[:, k * 256:(k + 1) * 256].rearrange("(k p) c -> p k c", p=128), writes=[("WDO", k)])
            tr.dma("pool", s_wfo[k], WFO[:, :, k * 256:(k + 1) * 256], wfo_d[:, k * 256:(k + 1) * 256].rearrange("(k p) c -> p k c", p=128), writes=[("WFO", k)])
        tr.dma("pool", sl[4], WO, wo_d.rearrange("(k p) c -> p k c", p=128), writes=["WO"])
        Hflat = view(0, [8, 4, D])
        for c in range(NCH):
            cs = slice(c * 512, (c + 1) * 512)
            for dc in range(8):
                for gi in range(2):
                    pb = gi
                    for kc in range(8):
                        wsrc = WGP[gi][:, kc, dc * 128:(dc + 1) * 128] if dc < 2 else WG[:, kc, gi * 1024 + dc * 128: gi * 1024 + (dc + 1) * 128]
                        wkey_ = ("GA" if gi == 0 else "GB") if dc < 2 else ("WG", gi, dc // 2)
                        tr.op("pe", lambda e, kc=kc, gi=gi, dc=dc, c=c, pb=pb: e.matmul(
                            PS[pb][:, :], lhsT=wsrc, rhs=H4[:, c, kc, :],
                            start=(kc == 0), stop=(kc == 7)),
                            reads=[wkey_, ("H", c)], writes=[("ps", pb)], inc=(kc == 7))
                for k in range(2):
                    tr.op("pe", lambda e, k=k, dc=dc, cs=cs: e.matmul(PS[2][:, :], lhsT=WDO[:, k, dc * 128:(dc + 1) * 128], rhs=OA[:, k, cs],
                                                                  start=(k == 0), stop=(k == 1)),
                          reads=[("WDO", dc // 2), "OAall"], writes=[("ps", 2)], inc=(k == 1))
                for k in range(4):
                    tr.op("pe", lambda e, k=k, dc=dc, cs=cs: e.matmul(PS[3][:, :], lhsT=WFO[:, k, dc * 128:(dc + 1) * 128], rhs=OB[:, k, cs],
                                                                  start=(k == 0), stop=(k == 3)),
                          reads=[("WFO", dc // 2), "OBall"], writes=[("ps", 3)], inc=(k == 3))
                for gi in range(2):
                    tr.op("act", lambda e, gi=gi, dc=dc: e.activation(out=SG[gi], in_=PS[gi][:, :], func=AF.Sigmoid,
                                                                     bias=bgate[:, gi * 8 + dc: gi * 8 + dc + 1], scale=1.0),
                          reads=[("ps", gi), "bgate"], writes=[("SG", gi)])
                    tr.op("dve", lambda e, gi=gi: e.tensor_tensor(out=M12[gi], in0=PS[2 + gi][:, :], in1=SG[gi], op=ALU.mult),
                          reads=[("ps", 2 + gi), ("SG", gi)], writes=[("M12", gi)])
                tr.op("dve", lambda e, dc=dc: e.tensor_tensor(out=MT[:, dc, :], in0=M12[0], in1=M12[1], op=ALU.add),
                      reads=[("M12", 0), ("M12", 1)], writes=["MT"])
            for tt in range(4):
                for dh in range(2):
                    pb = 4 + ((tt * 2 + dh) % 2)
                    for kc in range(8):
                        tr.op("pe", lambda e, kc=kc, tt=tt, dh=dh, pb=pb: e.matmul(
                            PS[pb][:, :], lhsT=MT[:, kc, tt * 128:(tt + 1) * 128], rhs=WO[:, kc, dh * 512:(dh + 1) * 512],
                            start=(kc == 0), stop=(kc == 7)),
                            reads=["WO", "MT"], writes=[("ps", pb)], inc=(kc == 7))
                    if dh == 0:
                        tr.op("dve", lambda e, c=c, tt=tt, dh=dh, pb=pb: e.tensor_copy(out=Hflat[:, c, tt, dh * 512:(dh + 1) * 512], in_=PS[pb][:, :]),
                              reads=[("ps", pb)], writes=[("H", c)])
                    else:
                        tr.op("act", lambda e, c=c, tt=tt, dh=dh, pb=pb: e.activation(out=Hflat[:, c, tt, dh * 512:(dh + 1) * 512], in_=PS[pb][:, :], func=AF.Copy),
                              reads=[("ps", pb)], writes=[("H", c)])
        tr.barrier()

        WD = view(92 * KB, [NJ, D])
        X2 = [view(136 * KB + i * 16 * KB, [4, D], F32) for i in range(2)]
        H2T = view(168 * KB, [8, 512])
        AT = view(176 * KB, [NJ, 512])
        H2N = [view(198 * KB + i * 2 * KB, [D]) for i in range(2)] + [view(170 * KB + i * 2 * KB, [D]) for i in range(2)]
        SGF = [view(202 * KB + i * 2 * KB, [512], F32) for i in range(2)]
        WS = [view(64 * KB + i * 8 * KB, [8, 2, 256]) for i in range(2)]
        s_ws = [tr.new_slot("ws%d" % i) for i in range(4)]
        s_x2 = [tr.new_slot("x2%d" % i) for i in range(8)]
        s_o = [tr.new_slot("o%d" % i) for i in range(8)]
        tr.dma("sp", s_g, GA, gffn_d.partition_broadcast(128), writes=["GA"])
        tr.dma("sp", tr.new_slot("gb2"), GB, gfin_d.partition_broadcast(128), writes=["GB"])
        out_evs = []
        JK = view(168 * KB, [D])

        def h2t_of(c):
            return H4[:, c]

        def h2t_keys(c):
            return [("H", c)]

        def prep_load(c):
            xb = X2[c % 2]
            for i in range(4):
                t = 4 * c + i
                sx = s_x2[(c % 2) * 4 + i]
                tr.dma("sp", sx, xb[:, i, :], x_d[t * 128:(t + 1) * 128, :], writes=[("X2", c % 2, i)])

        def prep(c):
            xb = X2[c % 2]
            for i in range(4):
                tr.op("dve", lambda e: e.tensor_tensor(out=xb[:, i, :], in0=xb[:, i, :], in1=Hflat[:, c, i, :], op=ALU.add),
                      reads=[("H", c)], writes=[("X2", c % 2, i)])
            for i in range(4):
                t = 4 * c + i
                tr.op("act", lambda e: e.activation(out=JK, in_=xb[:, i, :], func=AF.Square, accum_out=ss2[:, t:t + 1]),
                      reads=[("X2", c % 2, i)], writes=["JK", ("ss2", c)])
            tr.op("act", lambda e: e.activation(out=sd2[:, 4 * c:4 * c + 4], in_=ss2[:, 4 * c:4 * c + 4], func=AF.Sqrt, bias=epsb, scale=1.0 / D),
                  reads=[("ss2", c), "epsb"], writes=[("sd2", c)])
            tr.op("dve", lambda e: e.reciprocal(out=rstd2[:, 4 * c:4 * c + 4], in_=sd2[:, 4 * c:4 * c + 4]),
                  reads=[("sd2", c)], writes=[("rstd2", c)])
            for i in range(4):
                t = 4 * c + i
                b = i
                tr.op("dve", lambda e: e.scalar_tensor_tensor(
                    out=H2N[b], in0=xb[:, i, :], scalar=rstd2[:, t:t + 1], in1=GA, op0=ALU.mult, op1=ALU.mult),
                    reads=[("X2", c % 2, i), ("rstd2", c), "GA"], writes=[("H2N", b)])

        def prep_back(c):
            ht = h2t_of(c)
            for i in range(4):
                b = i
                pb = (6, 7, 0, 1)[i]
                for kc in range(8):
                    tr.op("pe", lambda e: e.transpose(
                        out=psb[pb][:, kc * 128:(kc + 1) * 128], in_=H2N[b][:, kc * 128:(kc + 1) * 128], identity=ident),
                        reads=[("H2N", b), "ident"], writes=[("ps", pb)], inc=(kc == 7))
                tr.op("dve", lambda e: e.tensor_copy(out=ht[:, :, i * 128:(i + 1) * 128], in_=psb[pb].rearrange("p (k t) -> p k t", k=8)),
                      reads=[("ps", pb)], writes=h2t_keys(c) + [("H2Tc", c)])

        def ffn_in(c, js_list):
            ht = h2t_of(c)
            for js in js_list:
                wsb = js % 2
                tr.dma("pool", s_ws[wsb * 2], WS[wsb][:, :, 0, :], wfi_d[:, js * 256:(js + 1) * 256].rearrange("(k p) c -> p k c", p=128), writes=[("WS", wsb, 0)])
                tr.dma("pool", s_ws[wsb * 2 + 1], WS[wsb][:, :, 1, :], wfi_d[:, DFF + js * 256: DFF + (js + 1) * 256].rearrange("(k p) c -> p k c", p=128), writes=[("WS", wsb, 1)])
                if c == 0:
                    tr.dma("pool", sl[0], WD[:, 2 * js:2 * js + 2, :],
                           wfd_d[2 * js * 128:(2 * js + 2) * 128, :].rearrange("(k p) c -> p k c", p=128), writes=["WD"])
                for jj in range(2):
                    j = 2 * js + jj
                    pg, pu = 2 + 2 * (j % 2), 3 + 2 * (j % 2)
                    for gu, pb in ((0, pg), (1, pu)):
                        for kc in range(8):
                            tr.op("pe", lambda e: e.matmul(
                                PS[pb][:, :], lhsT=WS[wsb][:, kc, gu, jj * 128:(jj + 1) * 128], rhs=ht[:, kc, :],
                                start=(kc == 0), stop=(kc == 7)),
                                reads=[("WS", wsb, gu), ("H2Tc", c)], writes=[("ps", pb)], inc=(kc == 7))
                    sb2 = j % 2
                    tr.op("act", lambda e: e.activation(out=SGF[sb2], in_=PS[pg][:, :], func=AF.Silu),
                          reads=[("ps", pg)], writes=[("SGF", sb2)])
                    tr.op("dve", lambda e: e.tensor_tensor(out=AT[:, j, :], in0=PS[pu][:, :], in1=SGF[sb2], op=ALU.mult),
                          reads=[("ps", pu), ("SGF", sb2)], writes=["AT"])

        def down_final(c):
            xb = X2[c % 2]
            per_tile = (c == NCH - 1)

            def fin_tiles(tiles):
                for i in tiles:
                    t = 4 * c + i
                    tr.op("act", lambda e: e.activation(out=JK, in_=xb[:, i, :], func=AF.Square, accum_out=ss3[:, t:t + 1]),
                          reads=[("X2", c % 2, i)], writes=["JK", ("ss3", c, i)])
                lo, hi = 4 * c + tiles[0], 4 * c + tiles[-1] + 1
                tr.op("act", lambda e: e.activation(out=sd3[:, lo:hi], in_=ss3[:, lo:hi], func=AF.Sqrt, bias=epsb, scale=1.0 / D),
                      reads=[("ss3", c, i) for i in tiles] + ["epsb"], writes=[("sd3", c, tiles[0])])
                tr.op("dve", lambda e: e.reciprocal(out=rstd3[:, lo:hi], in_=sd3[:, lo:hi]),
                      reads=[("sd3", c, tiles[0])], writes=[("rstd3", c, tiles[0])])
                for i in tiles:
                    t = 4 * c + i
                    tr.op("dve", lambda e: e.scalar_tensor_tensor(
                        out=xb[:, i, :], in0=xb[:, i, :], scalar=rstd3[:, t:t + 1], in1=GB, op0=ALU.mult, op1=ALU.mult),
                        reads=[("rstd3", c, tiles[0]), "GB"], writes=[("X2", c % 2, i)])
                    out_evs.append(tr.dma("sp", s_o[(c % 2) * 4 + i], out_d[t * 128:(t + 1) * 128, :], xb[:, i, :],
                                          reads=[("X2", c % 2, i)]))

            for i in range(4):
                for dh in range(2):
                    pb = (i * 2 + dh) % 2
                    for j in range(NJ):
                        tr.op("pe", lambda e: e.matmul(
                            PS[pb][:, :], lhsT=AT[:, j, i * 128:(i + 1) * 128], rhs=WD[:, j, dh * 512:(dh + 1) * 512],
                            start=(j == 0), stop=(j == NJ - 1)),
                            reads=["AT", "WD"], writes=[("ps", pb)], inc=(j == NJ - 1))
                    tr.op("dve", lambda e: e.tensor_tensor(
                        out=xb[:, i, dh * 512:(dh + 1) * 512], in0=PS[pb][:, :], in1=xb[:, i, dh * 512:(dh + 1) * 512], op=ALU.add),
                        reads=[("ps", pb)], writes=[("X2", c % 2, i)])
                if per_tile:
                    fin_tiles([i])
            if not per_tile:
                fin_tiles([0, 1, 2, 3])

        prep_load(0)
        prep(0)
        prep_back(0)
        for c in range(NCH):
            if c + 1 < NCH:
                prep_load(c + 1)
            ffn_in(c, range(0, 2))
            if c + 1 < NCH:
                prep(c + 1)
            ffn_in(c, range(2, 6))
            if c + 1 < NCH:
                prep_back(c + 1)
            ffn_in(c, range(6, 11))
            down_final(c)
        tr._emit_waits("sp", out_evs)
        tr.barrier()
        tr.emit()
    return nc


def _constants():
    bf = ml_dtypes.bfloat16
    ident = np.eye(128, dtype=np.float32).astype(bf)
    k = np.arange(128)[:, None]
    q = np.arange(128)[None, :]
    cmask = np.where(q >= k, 0.0, NEG).astype(np.float32).astype(bf)
    dilb = np.zeros((12, 128, 2, 256), dtype=bf)
    for g in range(3):
        d = DILS[g]
        for h in range(4):
            idx = g * 4 + h
            slope = 2.0 ** (-8.0 * (idx + 1) / 12.0)
            rel_prev = (q - k + 128).astype(np.float64)
            rel_cur = (q - k).astype(np.float64)
            for part, (rel, valid) in enumerate(((rel_cur, k <= q), (rel_prev, k >= q))):
                v = np.where(valid, -slope * rel * d * 8.0, NEG).astype(np.float32)
                hi = v.astype(bf)
                lo = (v - hi.astype(np.float32)).astype(bf)
                dilb[idx, :, 0, part * 128:(part + 1) * 128] = hi
                dilb[idx, :, 1, part * 128:(part + 1) * 128] = lo
    return ident, cmask, dilb


_CACHE = {}


def kernel(x, norm_mix_g, w_in, b_fgt, b_gate, w_dil_out, w_fox_out, w_out,
           norm_ffn_g, w_ffn_in, w_ffn_down, norm_final_g):
    f32 = lambda a: np.ascontiguousarray(np.asarray(a), dtype=np.float32)
    x = f32(x)
    if "nc" not in _CACHE:
        _CACHE["nc"] = build_program()
    nc = _CACHE["nc"]
    ident, cmask, dilb = _constants()
    shared = {
        "norm_mix_g": f32(norm_mix_g).reshape(1, D),
        "w_in": f32(w_in).reshape(D, IN_COLS),
        "b_fgt": f32(b_fgt).reshape(8, 1),
        "b_gate": f32(b_gate).reshape(16, 128),
        "w_dil_out": f32(w_dil_out).reshape(256, D),
        "w_fox_out": f32(w_fox_out).reshape(512, D),
        "w_out": f32(w_out).reshape(D, D),
        "norm_ffn_g": f32(norm_ffn_g).reshape(1, D),
        "w_ffn_in": f32(w_ffn_in).reshape(D, 2 * DFF),
        "w_ffn_down": f32(w_ffn_down).reshape(DFF, D),
        "norm_final_g": f32(norm_final_g).reshape(1, D),
        "c_ident": ident, "c_cmask": cmask, "c_dilb": dilb,
    }
    in_maps = []
    for b in range(8):
        m = dict(shared)
        m["x"] = x[b]
        in_maps.append(m)
    res = run_bass_kernel_spmd(nc, in_maps, core_ids=list(range(8)))
    return np.stack([np.asarray(r["out"], dtype=np.float32).reshape(T, D) for r in res.results], axis=0)
```
